# Optimizing a Trainium2 kernel written in Bass

```python
import math
import jax, jax.numpy as jnp
from jax import lax
import numpy as np

D_MODEL = 4096
BATCH = 8
SEQ = 2048
DEPTH = 1

CHUNK = 64
Q_BLOCK = 128
MLA_HEADS = 16
MLA_NOPE = 128
MLA_ROPE = 64
MLA_V = 128
Q_LORA = 1024
KV_LORA = 512
ROPE_THETA = 10000.0
SWA_HEADS = 32
SWA_KV_HEADS = 4
SWA_HEAD_DIM = 64
WINDOW = 128
SWA_BLOCK = 128
NUM_BUCKETS = 32
MAX_DISTANCE = 128
D_FF = 4 * D_MODEL
EPS = 1e-6
NEG = -1e30

IN_SIZES = (Q_LORA, KV_LORA, MLA_ROPE, SWA_HEADS * SWA_HEAD_DIM,
            SWA_KV_HEADS * SWA_HEAD_DIM, SWA_KV_HEADS * SWA_HEAD_DIM)
D_IN = sum(IN_SIZES)
MLA_WIDTH = MLA_HEADS * MLA_V
SWA_WIDTH = SWA_HEADS * SWA_HEAD_DIM

kernel_name = "hybrid_mla_swa_sink_gated_block"


def rmsnorm(x, g):
    xf = x.astype(jnp.float32)
    y = xf * lax.rsqrt(jnp.mean(xf * xf, axis=-1, keepdims=True) + EPS)
    return (y * g.astype(jnp.float32)).astype(x.dtype)


def rope_tables(positions, dim):
    half = dim // 2
    inv = ROPE_THETA ** (-jnp.arange(half, dtype=jnp.float32) * (2.0 / dim))
    ang = positions.astype(jnp.float32)[..., None] * inv
    return jnp.cos(ang), jnp.sin(ang)


def apply_rope(x, cos, sin):
    half = x.shape[-1] // 2
    xf = x.astype(jnp.float32)
    x1, x2 = xf[..., :half], xf[..., half:]
    return jnp.concatenate([x1 * cos - x2 * sin, x1 * sin + x2 * cos], axis=-1).astype(x.dtype)


def t5_bucket(rel):
    nb = NUM_BUCKETS // 2
    max_exact = nb // 2
    ret = jnp.where(rel > 0, nb, 0)
    n = jnp.abs(rel)
    nf = jnp.maximum(n, 1).astype(jnp.float32)
    large = max_exact + (jnp.log(nf / max_exact) / math.log(MAX_DISTANCE / max_exact)
                         * (nb - max_exact)).astype(jnp.int32)
    large = jnp.minimum(large, nb - 1)
    return ret + jnp.where(n < max_exact, n, large)


def mla_attention(q_lat, kv_lat, k_rope, positions, q_norm_g, kv_norm_g, w_uq, w_ukv):
    B, S, _ = q_lat.shape
    q = (rmsnorm(q_lat, q_norm_g) @ w_uq).reshape(B, S, MLA_HEADS, MLA_NOPE + MLA_ROPE)
    q_nope, q_pe = q[..., :MLA_NOPE], q[..., MLA_NOPE:]
    kv = (rmsnorm(kv_lat, kv_norm_g) @ w_ukv).reshape(B, S, MLA_HEADS, MLA_NOPE + MLA_V)
    k_nope, v = kv[..., :MLA_NOPE], kv[..., MLA_NOPE:]
    cos, sin = rope_tables(positions, MLA_ROPE)
    q_pe = apply_rope(q_pe, cos[:, :, None, :], sin[:, :, None, :])
    k_pe = apply_rope(k_rope, cos, sin)
    scale = (MLA_NOPE + MLA_ROPE) ** -0.5
    outs = []
    for i in range(S // Q_BLOCK):
        q0, kend = i * Q_BLOCK, (i + 1) * Q_BLOCK
        s = (jnp.einsum('bqhd,bkhd->bhqk', q_nope[:, q0:kend], k_nope[:, :kend])
             + jnp.einsum('bqhd,bkd->bhqk', q_pe[:, q0:kend], k_pe[:, :kend]))
        s = s.astype(jnp.float32) * scale
        qc = (q0 + jnp.arange(Q_BLOCK)) // CHUNK
        kc = jnp.arange(kend) // CHUNK
        s = jnp.where((kc[None, :] <= qc[:, None])[None, None], s, NEG)
        p = jax.nn.softmax(s, axis=-1).astype(v.dtype)
        outs.append(jnp.einsum('bhqk,bkhd->bqhd', p, v[:, :kend]))
    return jnp.concatenate(outs, axis=1).reshape(B, S, MLA_WIDTH)


def swa_attention(q, k, v, sinks, rel_bias):
    B, S, _ = q.shape
    NB = S // SWA_BLOCK
    G = SWA_HEADS // SWA_KV_HEADS
    qb = q.reshape(B, NB, SWA_BLOCK, SWA_KV_HEADS, G, SWA_HEAD_DIM)

    def band(t):
        t = t.reshape(B, S, SWA_KV_HEADS, SWA_HEAD_DIM)
        tp = jnp.pad(t, ((0, 0), (SWA_BLOCK, 0), (0, 0), (0, 0)))
        tp = tp.reshape(B, NB + 1, SWA_BLOCK, SWA_KV_HEADS, SWA_HEAD_DIM)
        return jnp.concatenate([tp[:, :-1], tp[:, 1:]], axis=2)

    kb, vb = band(k), band(v)
    s = jnp.einsum('bnqgrd,bnkgd->bngrqk', qb, kb).astype(jnp.float32) * SWA_HEAD_DIM ** -0.5
    iq = jnp.arange(SWA_BLOCK)
    ik = jnp.arange(2 * SWA_BLOCK) - SWA_BLOCK
    rel = ik[None, :] - iq[:, None]
    bias = rel_bias.astype(jnp.float32)[t5_bucket(rel)]
    bias = jnp.transpose(bias, (2, 0, 1)).reshape(SWA_KV_HEADS, G, SWA_BLOCK, 2 * SWA_BLOCK)
    blk = jnp.arange(NB)[:, None] * SWA_BLOCK
    qpos = blk + iq
    kpos = blk + ik
    qc = qpos[:, :, None] // CHUNK
    kc = kpos[:, None, :] // CHUNK
    valid = (kc <= qc) & (kc >= qc - WINDOW // CHUNK) & (kpos[:, None, :] >= 0)
    s = jnp.where(valid[None, :, None, None], s + bias, NEG)
    sk = sinks.astype(jnp.float32).reshape(1, 1, SWA_KV_HEADS, G, 1, 1)
    m = jnp.maximum(jnp.max(s, axis=-1, keepdims=True), sk)
    e = jnp.exp(s - m)
    p = e / (jnp.sum(e, axis=-1, keepdims=True) + jnp.exp(sk - m))
    o = jnp.einsum('bngrqk,bnkgd->bnqgrd', p.astype(v.dtype), vb)
    return o.reshape(B, S, SWA_WIDTH)


def setup_inputs(seed: int = 0) -> dict:
    key = jax.random.key(seed)
    ks = jax.random.split(key, 24)
    f32 = jnp.float32

    def nrm(k, shape, scale):
        return jax.random.normal(k, shape, f32) * scale

    x = nrm(ks[0], (BATCH, SEQ, D_MODEL), 1.0)
    c = nrm(ks[1], (BATCH, D_MODEL), 1.0)
    offset = jax.random.randint(ks[2], (BATCH, 1), 0, 64, dtype=jnp.int32) * CHUNK
    positions = (offset + jnp.arange(SEQ, dtype=jnp.int32)[None, :]).astype(jnp.int32)
    return {
        "x": x,
        "c": c,
        "positions": positions,
        "w_ada": nrm(ks[3], (DEPTH, D_MODEL, 6 * D_MODEL), 0.1 * D_MODEL ** -0.5),
        "b_ada": nrm(ks[4], (DEPTH, 6 * D_MODEL), 0.02),
        "pre_norm_g": 1.0 + nrm(ks[5], (DEPTH, 2, D_MODEL), 0.05),
        "post_norm_g": 1.0 + nrm(ks[6], (DEPTH, 2, D_MODEL), 0.05),
        "w_in": nrm(ks[7], (DEPTH, D_MODEL, D_IN), D_MODEL ** -0.5),
        "q_norm_g": 1.0 + nrm(ks[8], (DEPTH, Q_LORA), 0.05),
        "kv_norm_g": 1.0 + nrm(ks[9], (DEPTH, KV_LORA), 0.05),
        "w_uq": nrm(ks[10], (DEPTH, Q_LORA, MLA_HEADS * (MLA_NOPE + MLA_ROPE)), Q_LORA ** -0.5),
        "w_ukv": nrm(ks[11], (DEPTH, KV_LORA, MLA_HEADS * (MLA_NOPE + MLA_V)), KV_LORA ** -0.5),
        "swa_sinks": nrm(ks[12], (DEPTH, SWA_HEADS), 0.5),
        "rel_bias": nrm(ks[13], (NUM_BUCKETS, SWA_HEADS), 0.5),
        "w_gate": nrm(ks[14], (DEPTH, D_MODEL, 2 * D_MODEL), D_MODEL ** -0.5),
        "w_proj_a": nrm(ks[15], (DEPTH, MLA_WIDTH, D_MODEL), MLA_WIDTH ** -0.5),
        "w_proj_b": nrm(ks[16], (DEPTH, SWA_WIDTH, D_MODEL), SWA_WIDTH ** -0.5),
        "w_out": nrm(ks[17], (DEPTH, D_MODEL, D_MODEL), D_MODEL ** -0.5),
        "w_ff_up": nrm(ks[18], (DEPTH, D_MODEL, D_FF), D_MODEL ** -0.5),
        "w_ff_down": nrm(ks[19], (DEPTH, D_FF, D_MODEL), D_FF ** -0.5),
    }


def reference(x, c, positions, w_ada, b_ada, pre_norm_g, post_norm_g, w_in, q_norm_g,
              kv_norm_g, w_uq, w_ukv, swa_sinks, rel_bias, w_gate, w_proj_a, w_proj_b,
              w_out, w_ff_up, w_ff_down):
    offs = np.cumsum((0,) + IN_SIZES)
    c_act = jax.nn.silu(c)
    for l in range(DEPTH):
        mod = (c_act @ w_ada[l] + b_ada[l])[:, None, :]
        sh1, sc1, g1, sh2, sc2, g2 = jnp.split(mod, 6, axis=-1)

        h = rmsnorm(x, pre_norm_g[l, 0]) * (1.0 + sc1) + sh1
        proj = h @ w_in[l]
        q_lat, kv_lat, k_rope, q_s, k_s, v_s = [proj[..., offs[i]:offs[i + 1]] for i in range(6)]
        y_a = mla_attention(q_lat, kv_lat, k_rope, positions, q_norm_g[l], kv_norm_g[l],
                            w_uq[l], w_ukv[l]) @ w_proj_a[l]
        y_b = swa_attention(q_s, k_s, v_s, swa_sinks[l], rel_bias) @ w_proj_b[l]
        gates = jax.nn.sigmoid(h @ w_gate[l])
        g_a, g_b = gates[..., :D_MODEL], gates[..., D_MODEL:]
        y = (g_a * y_a + g_b * y_b) @ w_out[l]
        x = x + g1 * rmsnorm(y, post_norm_g[l, 0])

        h = rmsnorm(x, pre_norm_g[l, 1]) * (1.0 + sc2) + sh2
        y = jnp.square(jax.nn.relu(h @ w_ff_up[l])) @ w_ff_down[l]
        x = x + g2 * rmsnorm(y, post_norm_g[l, 1])
    return x
```

```python
import numpy as np
import ml_dtypes
from contextlib import ExitStack
import concourse.bass as bass
import concourse.mybir as mybir
from concourse.bass_utils import run_bass_kernel_spmd

F32, BF16, I32 = mybir.dt.float32, mybir.dt.bfloat16, mybir.dt.int32
AF = mybir.ActivationFunctionType
ALU = mybir.AluOpType
AX = mybir.AxisListType
ENG = ["pe", "act", "dve", "pool", "sp"]

D = 4096
T = 2048
NT = T // 128
NTB = T // 512
KC_D = D // 128
DFF = 4 * D
EPS = 1e-6
NEG = -1e30
MLA_HEADS = 16
SWA_HEADS = 32
MLA_SCALE = float((128 + 64) ** -0.5)
SWA_SCALE = float(64 ** -0.5)

CV_PRE0, CV_PRE1, CV_POST0, CV_POST1, CV_QG, CV_KVG, CV_BADA, CV_INV, CV_N = 0, 32, 64, 96, 128, 136, 140, 332, 333


class Sem:
    def __init__(self, h):
        self.h = h
        self.count = 0


class Rec:
    __slots__ = ("fn", "waits", "sig", "val", "amount")

    def __init__(self, fn, waits):
        self.fn = fn
        self.waits = [w for w in waits if w is not None]
        self.sig = None
        self.val = 0
        self.amount = 0


class Prog:
    def __init__(self, nc, es):
        self.nc = nc
        self.es = es
        self.q = {e: [] for e in ENG}
        self.psem = {e: self.new_sem("p_" + e) for e in ENG}
        self.dsems = []

    def new_sem(self, name):
        return Sem(self.es.enter_context(self.nc.semaphore(name)))

    def dma_sem(self, name):
        s = self.new_sem(name)
        self.dsems.append(s)
        return s

    def op(self, eng, fn, waits=(), sig=False):
        rec = Rec(fn, waits)
        if sig:
            s = self.psem[eng]
            s.count += 1
            rec.sig, rec.val, rec.amount = s, s.count, 1
        self.q[eng].append(rec)
        return rec

    def dma(self, eng, sem, fn, waits=()):
        rec = Rec(fn, waits)
        sem.count += 16
        rec.sig, rec.val, rec.amount = sem, sem.count, 16
        self.q[eng].append(rec)
        return rec

    def group(self, recs):
        m = max(r.val for r in recs)
        for r in recs:
            r.val = m
        return recs

    def barrier(self):
        class _W:
            pass
        dwaits = []
        for s in self.dsems:
            if s.count > 0:
                w = _W()
                w.sig, w.val = s, s.count
                dwaits.append(w)
        evs = []
        for e in ENG:
            if e in ("sp", "pool"):
                evs.append(self.op(e, lambda eng: eng.nop(), waits=dwaits, sig=True))
            else:
                q = self.q[e]
                if q and q[-1].sig is None:
                    s = self.psem[e]
                    s.count += 1
                    q[-1].sig, q[-1].val, q[-1].amount = s, s.count, 1
                if q:
                    evs.append(q[-1])
        for e in ENG:
            self.op(e, lambda eng: eng.nop(), waits=evs)
        return evs

    def replay(self, name, eng):
        waited = {}
        for rec in self.q[name]:
            for w in rec.waits:
                s, v = w.sig, w.val
                assert s is not None, "waiting on unsignaled op"
                if waited.get(id(s), 0) < v:
                    eng.wait_ge(s.h, v)
                    waited[id(s)] = v
            ins = rec.fn(eng)
            if rec.sig is not None:
                ins.then_inc(rec.sig.h, rec.amount)

    def run_block(self):
        nc = self.nc
        with nc.Block() as block:
            @block.tensor
            def _(e):
                self.replay("pe", e)

            @block.scalar
            def _(e):
                self.replay("act", e)

            @block.vector
            def _(e):
                self.replay("dve", e)

            @block.gpsimd
            def _(e):
                self.replay("pool", e)

            @block.sync
            def _(e):
                self.replay("sp", e)


DTSZ = {F32: 4, BF16: 2, I32: 4}


class Arena:
    def __init__(self, nc, name, nbytes):
        self.t = nc.alloc_sbuf_tensor(name, [128, nbytes // 2], BF16)
        self.nbytes = nbytes
        self.off = 0

    def reset(self, off=0):
        self.off = off

    def alloc(self, free_shape, dtype, parts=128):
        n = int(np.prod(free_shape))
        sz = n * DTSZ[dtype]
        sz_al = (sz + 63) // 64 * 64
        assert self.off + sz_al <= self.nbytes, f"arena overflow {self.off}+{sz_al}>{self.nbytes}"
        v = self.t[0:parts, self.off // 2:(self.off + sz) // 2]
        self.off += sz_al
        if dtype != BF16:
            v = v.bitcast(dtype)
        if len(free_shape) == 1:
            return v
        names = [f"d{i}" for i in range(len(free_shape))]
        pat = "p (" + " ".join(names) + ") -> p " + " ".join(names)
        kw = {nm: int(s) for nm, s in zip(names[1:], free_shape[1:])}
        return v.rearrange(pat, **kw)


class Ring:
    def __init__(self, bufs):
        self.bufs = bufs
        self.free = [[] for _ in bufs]
        self.i = 0

    def next(self):
        idx = self.i % len(self.bufs)
        self.i += 1
        return idx, self.bufs[idx], list(self.free[idx])

    def release(self, idx, evs):
        self.free[idx] = list(evs)


def wslab(W, c0, c1, n0, n1):
    return W.rearrange("(c p) n -> p c n", p=128)[:, c0:c1, n0:n1]


def build_nc(stop_after=99, debug=False):
    nc = bass.Bass("TRN2", target_bir_lowering=False)
    dk = "ExternalOutput" if debug else "Internal"

    def din(name, shape, dt):
        return nc.dram_tensor(name, shape, dt, kind="ExternalInput").ap()

    x_d = din("x", [T, D], F32)
    cT_d = din("cT", [128, 32], F32)
    pos_d = din("pos", [1, T], I32)
    colv_d = din("colv", [128, CV_N], F32)
    sinks_d = din("sinks", [1, 32], F32)
    biasg_d = din("biasg", [32, 128, 256], F32)
    maskc_d = din("maskc", [128, 256], F32)
    identf_d = din("identf", [128, 128], F32)
    mrow_d = din("mrow", [1, 256], F32)
    w_ada = din("w_ada", [D, 6 * D], F32)
    w_in = din("w_in_p", [D, 4224], F32)
    w_gate = din("w_gate", [D, 2 * D], F32)
    w_uq = din("w_uq_p", [1024, 4096], F32)
    w_ukv = din("w_ukv_p", [512, 4096], F32)
    w_pa = din("w_proj_a", [2048, D], F32)
    w_pb = din("w_proj_b", [2048, D], F32)
    w_out = din("w_out", [D, D], F32)
    w_up = din("w_ff_up", [D, DFF] if stop_after >= 8 else [128, 128], F32)
    w_dn = din("w_ff_down", [DFF, D] if stop_after >= 8 else [128, 128], F32)
    out_d = nc.dram_tensor("out", [T, D], F32, kind="ExternalOutput").ap()

    projT = nc.dram_tensor("projT", [33, 128, T], BF16, kind=dk).ap()
    vpad_d = nc.dram_tensor("vpad", [T, 1024], BF16, kind=dk).ap()
    gatesT = nc.dram_tensor("gatesT", [64, 128, T], BF16, kind=dk).ap()
    attnT = nc.dram_tensor("attnT", [32, 128, T], BF16, kind=dk).ap()
    zT = nc.dram_tensor("zT", [32, 128, T], BF16, kind=dk).ap()
    ytok = nc.dram_tensor("ytok", [T, D], BF16, kind=dk).ap()
    uT = nc.dram_tensor("uT", [128, 128, T], BF16, kind="Internal").ap()
    dbg_d = nc.dram_tensor("dbg", [128, 512], F32, kind=dk).ap()

    with ExitStack() as es:
        P = Prog(nc, es)
        ps = es.enter_context(nc.psum_tensor("ps", [128, 8, 512], F32))
        pers = Arena(nc, "pers", 6 * 1024)
        A = Arena(nc, "arena", 200 * 1024)

        colv = pers.alloc([CV_N], F32)
        modc = pers.alloc([192], F32)
        gs1 = pers.alloc([32], F32)
        gs2 = pers.alloc([32], F32)
        gv1 = pers.alloc([32], F32)
        gv2 = pers.alloc([32], F32)
        ssqp = pers.alloc([NT, 4], F32)
        rstdA = pers.alloc([NT], F32)
        stat = pers.alloc([NT, 4], F32)
        identf = pers.alloc([128], F32)
        identb = pers.alloc([128], BF16)
        onesf = pers.alloc([128], F32)
        onesb = pers.alloc([128], BF16)
        mrowb = pers.alloc([256], BF16, parts=1)
        sh1 = modc[:, 0:32]
        sc1 = modc[:, 32:64]
        g1c = modc[:, 64:96]
        sh2 = modc[:, 96:128]
        sc2 = modc[:, 128:160]
        g2c = modc[:, 160:192]

        s_const = P.dma_sem("s_const")
        cload = [
            P.dma("sp", s_const, lambda e: e.dma_start(out=colv, in_=colv_d)),
            P.dma("sp", s_const, lambda e: e.dma_start(out=identf, in_=identf_d)),
        ]
        P.group(cload)
        c_ib = P.op("dve", lambda e: e.tensor_copy(out=identb, in_=identf), waits=cload, sig=True)
        c_of = P.op("dve", lambda e: e.memset(onesf, 1.0), sig=True)
        c_ob = P.op("dve", lambda e: e.memset(onesb, 1.0), sig=True)
        consts_ready = cload + [c_ib, c_of, c_ob]

        A.reset()
        cTf = A.alloc([32], F32)
        cTb = A.alloc([32], BF16)
        mrowf = A.alloc([256], F32, parts=1)
        rowsb = Ring([A.alloc([512], F32, parts=1) for _ in range(2)])
        wada = Ring([A.alloc([32, 512], BF16) for _ in range(3)])
        s_c = P.dma_sem("s_c")
        l_c = P.dma("sp", s_c, lambda e: e.dma_start(out=cTf, in_=cT_d))
        l_m = P.dma("sp", s_c, lambda e: e.dma_start(out=mrowf, in_=mrow_d))
        P.group([l_c, l_m])
        c_silu = P.op("act", lambda e: e.activation(out=cTb, in_=cTf, func=AF.Silu), waits=[l_c], sig=True)
        c_mrow = P.op("dve", lambda e: e.tensor_copy(out=mrowb, in_=mrowf), waits=[l_m], sig=True)
        s_wada = [P.dma_sem(f"s_wada{i}") for i in range(3)]
        ps_row = Ring([ps[0:1, 0, :], ps[0:1, 1, :]])
        ps_col = Ring([ps[:, 2, 0:4], ps[:, 3, 0:4]])
        NJ = 48
        for j in range(NJ):
            wi, wb, wfree = wada.next()
            ld = P.dma("pool", s_wada[wi], lambda e, wb=wb, j=j: e.dma_start(out=wb, in_=wslab(w_ada, 0, 32, j * 512, (j + 1) * 512)), waits=wfree)
            ri, prow, prfree = ps_row.next()
            for kc in range(32):
                mm = P.op("pe", lambda e, prow=prow, wb=wb, kc=kc: e.matmul(prow, lhsT=cTb[:, kc:kc + 1], rhs=wb[:, kc, :], start=(kc == 0), stop=(kc == 31)),
                          waits=([ld, c_silu] + prfree) if kc == 0 else (), sig=(kc == 31))
            wada.release(wi, [mm])
            si, rsb, rsfree = rowsb.next()
            cp = P.op("act", lambda e, rsb=rsb, prow=prow: e.activation(out=rsb, in_=prow, func=AF.Copy), waits=[mm] + rsfree, sig=True)
            ps_row.release(ri, [cp])
            ci, pcol, pcfree = ps_col.next()
            for i in range(4):
                k1 = P.op("pe", lambda e, pcol=pcol, rsb=rsb, i=i: e.matmul(pcol[:, i:i + 1], lhsT=rsb[0:1, i * 128:(i + 1) * 128], rhs=onesf[0:1, 0:1], start=True, stop=True),
                          waits=([cp, c_of] + pcfree) if i == 0 else (), sig=(i == 3))
            rowsb.release(si, [k1])
            ad = P.op("dve", lambda e, pcol=pcol, j=j: e.tensor_tensor(out=modc[:, j * 4:(j + 1) * 4], in0=pcol, in1=colv[:, CV_BADA + j * 4:CV_BADA + (j + 1) * 4], op=ALU.add),
                      waits=[k1] + cload, sig=True)
            ps_col.release(ci, [ad])
        m1 = P.op("dve", lambda e: e.scalar_tensor_tensor(out=gs1, in0=sc1, scalar=1.0, in1=colv[:, CV_PRE0:CV_PRE0 + 32], op0=ALU.add, op1=ALU.mult), waits=[ad], sig=True)
        m2 = P.op("dve", lambda e: e.scalar_tensor_tensor(out=gs2, in0=sc2, scalar=1.0, in1=colv[:, CV_PRE1:CV_PRE1 + 32], op0=ALU.add, op1=ALU.mult), waits=[ad], sig=True)
        m3 = P.op("dve", lambda e: e.tensor_tensor(out=gv1, in0=g1c, in1=colv[:, CV_POST0:CV_POST0 + 32], op=ALU.mult), waits=[ad], sig=True)
        m4 = P.op("dve", lambda e: e.tensor_tensor(out=gv2, in0=g2c, in1=colv[:, CV_POST1:CV_POST1 + 32], op=ALU.mult), waits=[ad], sig=True)
        mod_ready = [m1, m2, m3, m4, c_mrow]
        P.barrier()
        if debug:
            s_dbg = P.dma_sem("s_dbg")
            P.dma("sp", s_dbg, lambda e: e.dma_start(out=dbg_d[:, 0:192], in_=modc))
            P.dma("sp", s_dbg, lambda e: e.dma_start(out=dbg_d[:, 192:224], in_=gs1))

        pst_ring = Ring([ps[:, b, :] for b in range(8)])

        def norm_transpose(tt, xt, xt_ready, junk, junk_free, xh, xh_free, gs, sh, hT):
            sq = P.op("act", lambda e: e.activation(out=junk, in_=xt, func=AF.Square, accum_out=stat[:, tt, 0:1]), waits=xt_ready + junk_free, sig=True)
            t1 = P.op("dve", lambda e: e.tensor_scalar(out=stat[:, tt, 1:2], in0=stat[:, tt, 0:1], scalar1=1.0 / D, scalar2=EPS, op0=ALU.mult, op1=ALU.add), waits=[sq], sig=True)
            t2 = P.op("act", lambda e: e.activation(out=stat[:, tt, 2:3], in_=stat[:, tt, 1:2], func=AF.Sqrt), waits=[t1], sig=True)
            t3 = P.op("dve", lambda e: e.reciprocal(out=stat[:, tt, 3:4], in_=stat[:, tt, 2:3]), waits=[t2], sig=True)
            sc = P.op("dve", lambda e: e.tensor_scalar(out=xh, in0=xt, scalar1=stat[:, tt, 3:4], scalar2=0.0, op0=ALU.mult, op1=ALU.add), waits=[t3] + xt_ready + xh_free, sig=True)
            last_tr = None
            for g in range(8):
                bi, pb, pfree = pst_ring.next()
                for q in range(4):
                    c = g * 4 + q
                    last_tr = P.op("pe", lambda e, pb=pb, q=q, c=c: e.transpose(pb[:, q * 128:(q + 1) * 128], xh[:, c * 128:(c + 1) * 128], identf),
                                   waits=([sc] + pfree + cload) if q == 0 else (), sig=(q == 3))
                for q in range(4):
                    c = g * 4 + q
                    ev = P.op("act", lambda e, pb=pb, q=q, c=c: e.activation(out=hT[:, c, tt * 128:(tt + 1) * 128], in_=pb[:, q * 128:(q + 1) * 128], func=AF.Identity,
                                                                            scale=gs[:, c:c + 1], bias=sh[:, c:c + 1]),
                              waits=([last_tr] + mod_ready) if q == 0 else (), sig=(q == 3))
                pst_ring.release(bi, [ev])
            return [sq, sc], [last_tr], ev

        A.reset()
        hT = A.alloc([32, T], BF16)
        off_after_hT = A.off
        xts = Ring([A.alloc([D], F32) for _ in range(2)])
        xh = A.alloc([D], F32)
        junk = A.alloc([D], BF16)
        s_x = [P.dma_sem(f"s_x{i}") for i in range(2)]
        xh_free, junk_free = [], []
        h_ready = []
        for tt in range(NT):
            xi, xt, xfree = xts.next()
            ld = P.dma("sp", s_x[xi], lambda e, xt=xt, tt=tt: e.dma_start(out=xt, in_=x_d[tt * 128:(tt + 1) * 128, :]), waits=xfree)
            rd, xhf, ev = norm_transpose(tt, xt, [ld], junk, junk_free, xh, xh_free, gs1, sh1, hT)
            xts.release(xi, rd)
            xh_free, junk_free = xhf, [rd[0]]
            h_ready = [ev]
        P.barrier()
        if stop_after <= 1:
            if debug:
                P.dma("sp", s_dbg, lambda e: e.dma_start(out=projT[0:32].rearrange("c p t -> p c t"), in_=hT))
            P.barrier()
            P.run_block()
            return nc

        ps_ring8 = Ring([ps[:, b, :] for b in range(8)])

        def gemm_b(name, inT, KC, W, col0, nchunks, wring, wsems, epi, rows_last=128, slab=2):
            nslab = (nchunks + slab - 1) // slab
            for s in range(nslab):
                c_lo = s * slab
                c_hi = min(nchunks, c_lo + slab)
                ncol = (c_hi - c_lo) * 128
                wi, wb, wfree = wring.next()
                ld = P.dma("pool", wsems[wi], lambda e, wb=wb, c_lo=c_lo, ncol=ncol: e.dma_start(out=wb[:, :, 0:ncol], in_=wslab(W, 0, KC, col0 + c_lo * 128, col0 + c_lo * 128 + ncol)), waits=wfree)
                mm = None
                for ci in range(c_lo, c_hi):
                    M = rows_last if ci == nchunks - 1 else 128
                    lo = (ci - c_lo) * 128
                    for tb in range(NTB):
                        bi, pb, pfree = ps_ring8.next()
                        for kc in range(KC):
                            mm = P.op("pe", lambda e, pb=pb, wb=wb, lo=lo, M=M, kc=kc, tb=tb: e.matmul(pb[0:M, :], lhsT=wb[:, kc, lo:lo + M], rhs=inT[:, kc, tb * 512:(tb + 1) * 512], start=(kc == 0), stop=(kc == KC - 1)),
                                      waits=([ld] + pfree) if kc == 0 else (), sig=(kc == KC - 1))
                        fr = epi(ci, tb, pb, mm)
                        ps_ring8.release(bi, [fr])
                wring.release(wi, [mm])

        A.reset(off_after_hT)
        w2 = Ring([A.alloc([32, 256], BF16) for _ in range(2)])
        s_w2 = [P.dma_sem(f"s_w2_{i}") for i in range(2)]
        stg = Ring([A.alloc([T], BF16) for _ in range(3)])
        s_st = [P.dma_sem(f"s_st{i}") for i in range(3)]
        vst = Ring([A.alloc([1024], BF16) for _ in range(2)])
        s_vst = [P.dma_sem(f"s_vst{i}") for i in range(2)]
        cur = {}

        def make_epi(dst, func):
            def epi(ci, tb, pb, mm):
                if tb == 0:
                    cur["i"], cur["buf"], cur["free"] = stg.next()
                    cur["evs"] = []
                buf = cur["buf"]
                w = [mm] + (cur["free"] if tb == 0 else [])
                if func is None:
                    ev = P.op("dve", lambda e: e.tensor_copy(out=buf[:, tb * 512:(tb + 1) * 512], in_=pb), waits=w, sig=True)
                else:
                    ev = P.op("act", lambda e: e.activation(out=buf[:, tb * 512:(tb + 1) * 512], in_=pb, func=func), waits=w, sig=True)
                cur["evs"].append(ev)
                if tb == NTB - 1:
                    i = cur["i"]
                    st = P.dma("sp", s_st[i], lambda e: e.dma_start(out=dst[ci], in_=buf), waits=cur["evs"])
                    stg.release(i, [st])
                return ev
            return epi

        import os as _os
        _p2n = int(_os.environ.get("K_P2N", "30"))
        _p2rest = _os.environ.get("K_P2REST", "1") == "1"
        epi_in = make_epi(projT, None)
        gemm_b("w_in_a", hT, 32, w_in, 0, _p2n, w2, s_w2, epi_in)
        if _os.environ.get("K_P2B", "1") == "1":
            gemm_b("w_in_b", hT, 32, w_in, 32 * 128, 1, w2, s_w2, lambda ci, tb, pb, mm: epi_in(32, tb, pb, mm))
        _v_on = _os.environ.get("K_P2V", "1") == "1"
        if not _v_on:
            stg_save = None
        wi, wb, wfree = w2.next() if _v_on else (0, w2.bufs[0], [])
        ldv = None if not _v_on else P.dma("pool", s_w2[wi], lambda e, wb=wb: e.dma_start(out=wb, in_=wslab(w_in, 0, 32, 30 * 128, 32 * 128)), waits=wfree)
        vz = [P.op("dve", lambda e, b=b: e.memset(b, 0.0), sig=True) for b in vst.bufs] if _v_on else []
        for tt in range(NT if _os.environ.get("K_P2V", "1") == "1" else 0):
            bi, pb, pfree = ps_ring8.next()
            for kc in range(32):
                mm = P.op("pe", lambda e, pb=pb, kc=kc, tt=tt, wb=wb, hT=hT: e.matmul(pb[:, 0:256], lhsT=hT[:, kc, tt * 128:(tt + 1) * 128], rhs=wb[:, kc, :], start=(kc == 0), stop=(kc == 31)),
                          waits=([ldv] + pfree) if kc == 0 else (), sig=(kc == 31))
            vi, vb, vfree = vst.next()
            vb4 = vb.rearrange("p (g r d) -> p g r d", g=4, r=2)
            e0 = P.op("dve", lambda e, vb4=vb4, pb=pb: e.tensor_copy(out=vb4[:, :, 0, 0:64], in_=pb[:, 0:256].rearrange("p (g d) -> p g d", g=4)), waits=[mm] + vfree + vz, sig=True)
            e1 = P.op("dve", lambda e, vb4=vb4, pb=pb: e.tensor_copy(out=vb4[:, :, 1, 64:128], in_=pb[:, 0:256].rearrange("p (g d) -> p g d", g=4)), waits=[e0], sig=True)
            ps_ring8.release(bi, [e0, e1])
            st = P.dma("sp", s_vst[vi], lambda e, vb=vb, tt=tt: e.dma_start(out=vpad_d[tt * 128:(tt + 1) * 128, :], in_=vb), waits=[e0, e1])
            vst.release(vi, [st])
        if _v_on:
            w2.release(wi, [mm])
        gemm_b("w_gate", hT, 32, w_gate, 0, int(_os.environ.get("K_P2G", "64")), w2, s_w2, make_epi(gatesT, {"none": None, "copy": AF.Copy, "sigmoid": AF.Sigmoid}[_os.environ.get("K_GATEF", "sigmoid")]))
        P.barrier()
        if stop_after <= 2:
            P.run_block()
            return nc

        def run_skewed(n_units, stageA, stageB, stageC, single_buf=False):
            if single_buf:
                for n in range(n_units + 1):
                    if n < n_units:
                        stageA(n)
                        stageB(n)
                    if 0 <= n - 1 < n_units:
                        stageC(n - 1)
                return
            for n in range(n_units + 2):
                if n < n_units:
                    stageA(n)
                if 0 <= n - 1 < n_units:
                    stageB(n - 1)
                if 0 <= n - 2 < n_units:
                    stageC(n - 2)

        A.reset()
        qlT = A.alloc([8, T], BF16)
        kvlT = A.alloc([4, T], BF16)
        kx = A.alloc([T], BF16)
        ksw = A.alloc([T], BF16)
        kpe2 = A.alloc([T], BF16)
        cos2 = A.alloc([T], F32)
        sinS = A.alloc([T], F32)
        off_grp = A.off
        s_l3 = P.dma_sem("s_l3")
        l_q = P.dma("sp", s_l3, lambda e: e.dma_start(out=qlT, in_=projT[0:8].rearrange("c p t -> p c t")))
        l_kv = P.dma("sp", s_l3, lambda e: e.dma_start(out=kvlT, in_=projT[8:12].rearrange("c p t -> p c t")))
        l_k = [P.dma("sp", s_l3, lambda e: e.dma_start(out=kx[0:64, :], in_=projT[32, 0:64, :])),
               P.dma("sp", s_l3, lambda e: e.dma_start(out=kx[64:128, :], in_=projT[32, 0:64, :])),
               P.dma("sp", s_l3, lambda e: e.dma_start(out=ksw[0:64, :], in_=projT[32, 64:128, :])),
               P.dma("sp", s_l3, lambda e: e.dma_start(out=ksw[64:128, :], in_=projT[32, 64:128, :]))]
        posi = A.alloc([T], I32)
        ang = A.alloc([T], F32)
        kf = A.alloc([T], F32)
        ki = A.alloc([T], I32)
        rbc = A.alloc([T], F32)
        sqb = Ring([A.alloc([T], BF16) for _ in range(2)])
        l_pos = P.dma("sp", s_l3, lambda e: e.dma_start(out=posi, in_=pos_d.partition_broadcast(128)))
        L3 = P.group([l_q, l_kv, l_pos] + l_k)

        def latent_norm(xT, nch, gcol0, ready):
            evs = []
            for c in range(nch):
                si, sb, sfree = sqb.next()
                sq = P.op("act", lambda e, sb=sb, c=c: e.activation(out=sb, in_=xT[:, c, :], func=AF.Square), waits=ready + sfree, sig=True)
                for tb in range(NTB):
                    mm = P.op("pe", lambda e, sb=sb, tb=tb, c=c: e.matmul(ps[:, tb, :], lhsT=onesb, rhs=sb[:, tb * 512:(tb + 1) * 512], start=(c == 0), stop=(c == nch - 1)),
                              waits=[sq, c_ob] if tb == 0 else (), sig=(tb == NTB - 1))
                sqb.release(si, [mm])
            t1 = P.op("dve", lambda e: e.tensor_scalar(out=rbc, in0=ps[:, 0:4, :].rearrange("p b n -> p (b n)"), scalar1=1.0 / (nch * 128), scalar2=EPS, op0=ALU.mult, op1=ALU.add), waits=[mm], sig=True)
            t2 = P.op("act", lambda e: e.activation(out=rbc, in_=rbc, func=AF.Sqrt), waits=[t1], sig=True)
            t3 = P.op("dve", lambda e: e.reciprocal(out=rbc, in_=rbc), waits=[t2], sig=True)
            ev = t3
            for c in range(nch):
                ev = P.op("dve", lambda e, c=c: e.scalar_tensor_tensor(out=xT[:, c, :], in0=xT[:, c, :], scalar=colv[:, gcol0 + c:gcol0 + c + 1], in1=rbc, op0=ALU.mult, op1=ALU.mult),
                          waits=[t3] + cload, sig=True)
            return ev

        n_q = latent_norm(qlT, 8, CV_QG, [l_q])
        n_kv = latent_norm(kvlT, 4, CV_KVG, [l_kv, n_q])

        C1 = 6.28125
        C2 = float(2 * np.pi - 6.28125)
        PI = float(np.pi)
        r0 = P.op("dve", lambda e: e.tensor_copy(out=ang, in_=posi), waits=[l_pos], sig=True)
        r1 = P.op("dve", lambda e: e.tensor_scalar(out=ang, in0=ang, scalar1=colv[:, CV_INV:CV_INV + 1], scalar2=0.0, op0=ALU.mult, op1=ALU.add), waits=[r0] + cload, sig=True)

        def sin_of(dst, shift, prev):
            a = P.op("dve", lambda e: e.tensor_scalar(out=dst, in0=ang, scalar1=shift, scalar2=0.0, op0=ALU.add, op1=ALU.add), waits=prev, sig=True)
            b = P.op("dve", lambda e: e.tensor_scalar(out=kf, in0=dst, scalar1=float(1 / (2 * np.pi)), scalar2=0.0, op0=ALU.mult, op1=ALU.add), waits=[a], sig=True)
            c = P.op("dve", lambda e: e.tensor_copy(out=ki, in_=kf), waits=[b], sig=True)
            d = P.op("dve", lambda e: e.tensor_copy(out=kf, in_=ki), waits=[c], sig=True)
            f = P.op("dve", lambda e: e.scalar_tensor_tensor(out=dst, in0=kf, scalar=-C1, in1=dst, op0=ALU.mult, op1=ALU.add), waits=[d], sig=True)
            g = P.op("dve", lambda e: e.scalar_tensor_tensor(out=dst, in0=kf, scalar=-C2, in1=dst, op0=ALU.mult, op1=ALU.add), waits=[f], sig=True)
            h = P.op("dve", lambda e: e.tensor_scalar(out=kf, in0=dst, scalar1=PI, scalar2=0.0, op0=ALU.is_gt, op1=ALU.add), waits=[g], sig=True)
            i = P.op("dve", lambda e: e.scalar_tensor_tensor(out=dst, in0=kf, scalar=-2 * PI, in1=dst, op0=ALU.mult, op1=ALU.add), waits=[h], sig=True)
            h2 = P.op("dve", lambda e: e.tensor_scalar(out=kf, in0=dst, scalar1=-PI, scalar2=0.0, op0=ALU.is_lt, op1=ALU.add), waits=[i], sig=True)
            i2 = P.op("dve", lambda e: e.scalar_tensor_tensor(out=dst, in0=kf, scalar=2 * PI, in1=dst, op0=ALU.mult, op1=ALU.add), waits=[h2], sig=True)
            k = P.op("dve", lambda e: e.tensor_scalar(out=dst, in0=dst, scalar1=-PI, scalar2=PI, op0=ALU.max, op1=ALU.min), waits=[i2], sig=True)
            return P.op("act", lambda e: e.activation(out=dst, in_=dst, func=AF.Sin), waits=[k], sig=True)

        e_sin = sin_of(sinS, 0.0, [r1, n_kv])
        e_cos = sin_of(cos2, PI / 2, [e_sin])
        e_s1 = P.op("dve", lambda e: e.tensor_scalar(out=sinS[0:32, :], in0=sinS[0:32, :], scalar1=-1.0, scalar2=0.0, op0=ALU.mult, op1=ALU.add), waits=[e_sin, e_cos], sig=True)
        e_s2 = P.op("dve", lambda e: e.tensor_scalar(out=sinS[64:96, :], in0=sinS[64:96, :], scalar1=-1.0, scalar2=0.0, op0=ALU.mult, op1=ALU.add), waits=[e_s1], sig=True)
        e_k1 = P.op("dve", lambda e: e.tensor_tensor(out=ang, in0=kx, in1=cos2, op=ALU.mult), waits=l_k + [e_s2], sig=True)
        e_k2 = P.op("dve", lambda e: e.tensor_tensor(out=kf, in0=ksw, in1=sinS, op=ALU.mult), waits=l_k + [e_k1], sig=True)
        e_kpe = P.op("dve", lambda e: e.tensor_tensor(out=kpe2, in0=ang, in1=kf, op=ALU.add), waits=[e_k2], sig=True)
        setup3 = [e_kpe, n_kv, n_q, e_s2, e_cos]

        A.reset(off_grp)
        qn = A.alloc([4, T], BF16)
        qp = A.alloc([2, T], BF16)
        kn = A.alloc([4, T], BF16)
        vt = A.alloc([NT, 512], BF16)
        wq_n = A.alloc([8, 512], BF16)
        wq_r = A.alloc([8, 256], BF16)
        wq_s = A.alloc([8, 256], BF16)
        wk_k = A.alloc([4, 512], BF16)
        wk_v = A.alloc([4, 512], BF16)
        rt1 = Ring([A.alloc([512], F32) for _ in range(2)])
        rt2 = Ring([A.alloc([512], F32) for _ in range(2)])
        Pb = Ring([A.alloc([T], BF16) for _ in range(2)])
        pTb = Ring([A.alloc([NT, 128], BF16) for _ in range(2)])
        dg = Ring([A.alloc([128], BF16) for _ in range(3)])
        sm = Ring([A.alloc([4], F32) for _ in range(4)])
        ast = Ring([A.alloc([T], BF16) for _ in range(2)])
        s_ast = [P.dma_sem(f"s_ast{i}") for i in range(2)]
        s_wg = P.dma_sem("s_wg")
        ps_sc = ps[:, 0:4, :].rearrange("p b n -> p (b n)")
        ps_t = Ring([ps[:, 4, :], ps[:, 5, :]])
        ps_o = Ring([ps[:, 6, 0:128], ps[:, 7, 0:128]])
        ps_g = Ring([ps[:, 4, :], ps[:, 5, :], ps[:, 6, :], ps[:, 7, :]])
        sc_free = [[]]
        grp_free = setup3
        ast_state = {}

        for g in range(4):
            wl = [P.dma("pool", s_wg, lambda e, g=g: e.dma_start(out=wq_n, in_=wslab(w_uq, 0, 8, g * 512, (g + 1) * 512)), waits=grp_free),
                  P.dma("pool", s_wg, lambda e, g=g: e.dma_start(out=wq_r, in_=wslab(w_uq, 0, 8, 2048 + g * 256, 2048 + (g + 1) * 256))),
                  P.dma("pool", s_wg, lambda e, g=g: e.dma_start(out=wq_s, in_=wslab(w_uq, 0, 8, 3072 + g * 256, 3072 + (g + 1) * 256))),
                  P.dma("pool", s_wg, lambda e, g=g: e.dma_start(out=wk_k, in_=wslab(w_ukv, 0, 4, g * 512, (g + 1) * 512))),
                  P.dma("pool", s_wg, lambda e, g=g: e.dma_start(out=wk_v, in_=wslab(w_ukv, 0, 4, 2048 + g * 512, 2048 + (g + 1) * 512)))]
            P.group(wl)
            gw = wl + grp_free
            gevs = []
            for (dst, wsrc, src, KC) in ((qn, wq_n, qlT, 8), (kn, wk_k, kvlT, 4)):
                for hh in range(4):
                    for tb in range(NTB):
                        bi, pb, pfree = ps_g.next()
                        for kc in range(KC):
                            mm = P.op("pe", lambda e, pb=pb, wsrc=wsrc, src=src, hh=hh, kc=kc, tb=tb, KC=KC: e.matmul(pb, lhsT=wsrc[:, kc, hh * 128:(hh + 1) * 128], rhs=src[:, kc, tb * 512:(tb + 1) * 512], start=(kc == 0), stop=(kc == KC - 1)),
                                      waits=(gw + pfree) if kc == 0 else (), sig=(kc == KC - 1))
                        ev = P.op("act", lambda e, pb=pb, dst=dst, hh=hh, tb=tb: e.activation(out=dst[:, hh, tb * 512:(tb + 1) * 512], in_=pb, func=AF.Copy), waits=[mm], sig=True)
                        ps_g.release(bi, [ev])
                        gevs.append(ev)
            for pc in range(2):
                for tb in range(NTB):
                    b1, pb1, pf1 = ps_g.next()
                    for kc in range(8):
                        mm1 = P.op("pe", lambda e, pb1=pb1, pc=pc, kc=kc, tb=tb: e.matmul(pb1, lhsT=wq_r[:, kc, pc * 128:(pc + 1) * 128], rhs=qlT[:, kc, tb * 512:(tb + 1) * 512], start=(kc == 0), stop=(kc == 7)),
                                    waits=(gw + pf1) if kc == 0 else (), sig=(kc == 7))
                    b2, pb2, pf2 = ps_g.next()
                    for kc in range(8):
                        mm2 = P.op("pe", lambda e, pb2=pb2, pc=pc, kc=kc, tb=tb: e.matmul(pb2, lhsT=wq_s[:, kc, pc * 128:(pc + 1) * 128], rhs=qlT[:, kc, tb * 512:(tb + 1) * 512], start=(kc == 0), stop=(kc == 7)),
                                    waits=(gw + pf2) if kc == 0 else (), sig=(kc == 7))
                    i1, t1b, f1 = rt1.next()
                    i2, t2b, f2 = rt2.next()
                    a1 = P.op("dve", lambda e, t1b=t1b, pb1=pb1, tb=tb: e.tensor_tensor(out=t1b, in0=pb1, in1=cos2[:, tb * 512:(tb + 1) * 512], op=ALU.mult), waits=[mm1] + f1, sig=True)
                    a2 = P.op("dve", lambda e, t2b=t2b, pb2=pb2, tb=tb: e.tensor_tensor(out=t2b, in0=pb2, in1=sinS[:, tb * 512:(tb + 1) * 512], op=ALU.mult), waits=[mm2] + f2, sig=True)
                    a3 = P.op("dve", lambda e, t1b=t1b, t2b=t2b, pc=pc, tb=tb: e.tensor_tensor(out=qp[:, pc, tb * 512:(tb + 1) * 512], in0=t1b, in1=t2b, op=ALU.add), waits=[a1, a2], sig=True)
                    ps_g.release(b1, [a1])
                    ps_g.release(b2, [a2])
                    rt1.release(i1, [a3])
                    rt2.release(i2, [a3])
                    gevs.append(a3)
            for tt in range(NT):
                bi, pb, pfree = ps_g.next()
                for kc in range(4):
                    mm = P.op("pe", lambda e, pb=pb, kc=kc, tt=tt: e.matmul(pb, lhsT=kvlT[:, kc, tt * 128:(tt + 1) * 128], rhs=wk_v[:, kc, :], start=(kc == 0), stop=(kc == 3)),
                              waits=(gw + pfree) if kc == 0 else (), sig=(kc == 3))
                ev = P.op("act", lambda e, pb=pb, tt=tt: e.activation(out=vt[:, tt, :], in_=pb, func=AF.Copy), waits=[mm], sig=True)
                ps_g.release(bi, [ev])
                gevs.append(ev)
            grp_ready = gevs

            units = [(hh, i) for hh in range(4) for i in range(NT)]
            U = {}

            def stA(n, g=g, units=units, U=U, grp_ready=grp_ready):
                hh, i = units[n]
                L = (i + 1) * 128
                r0_ = (hh % 2) * 64
                first = True
                segs = [(s0, min(512, i * 128 - s0)) for s0 in range(0, i * 128, 512)]
                mm = None
                for (s0, n_) in segs:
                    P.op("pe", lambda e, hh=hh, i=i, s0=s0, n_=n_: e.matmul(ps_sc[:, s0:s0 + n_], lhsT=qn[:, hh, i * 128:(i + 1) * 128], rhs=kn[:, hh, s0:s0 + n_], start=True, stop=False),
                         waits=(grp_ready + sc_free[0]) if first else ())
                    first = False
                    mm = P.op("pe", lambda e, hh=hh, i=i, s0=s0, n_=n_, r0_=r0_: e.matmul(ps_sc[:, s0:s0 + n_], lhsT=qp[r0_:r0_ + 64, hh // 2, i * 128:(i + 1) * 128], rhs=kpe2[r0_:r0_ + 64, s0:s0 + n_], start=False, stop=True))
                d0 = i * 128
                P.op("pe", lambda e, hh=hh, i=i, d0=d0: e.matmul(ps_sc[:, d0:d0 + 128], lhsT=qn[:, hh, i * 128:(i + 1) * 128], rhs=kn[:, hh, d0:d0 + 128], start=True, stop=False),
                     waits=(grp_ready + sc_free[0]) if first else ())
                P.op("pe", lambda e, hh=hh, i=i, d0=d0, r0_=r0_: e.matmul(ps_sc[:, d0:d0 + 128], lhsT=qp[r0_:r0_ + 64, hh // 2, i * 128:(i + 1) * 128], rhs=kpe2[r0_:r0_ + 64, d0:d0 + 128], start=False, stop=False))
                mm = P.op("pe", lambda e, d0=d0: e.matmul(ps_sc[:, d0:d0 + 128], lhsT=mrowb[0:1, 0:128], rhs=mrowb[0:1, 128:256], start=False, stop=True), waits=mod_ready, sig=True)
                U[n] = {"S": mm, "L": L}

            def stB(n, units=units, U=U):
                hh, i = units[n]
                L = U[n]["L"]
                si, sb, sfree = sm.next()
                mx = P.op("dve", lambda e, sb=sb, L=L: e.reduce_max(out=sb[:, 0:1], in_=ps_sc[:, 0:L], axis=AX.X), waits=[U[n]["S"]] + sfree, sig=True)
                nb_ = P.op("dve", lambda e, sb=sb: e.tensor_scalar(out=sb[:, 1:2], in0=sb[:, 0:1], scalar1=-MLA_SCALE, scalar2=0.0, op0=ALU.mult, op1=ALU.add), waits=[mx], sig=True)
                pi_, pb_, pfree = Pb.next()
                ex = P.op("act", lambda e, sb=sb, pb_=pb_, L=L: e.activation(out=pb_[:, 0:L], in_=ps_sc[:, 0:L], func=AF.Exp, bias=sb[:, 1:2], scale=MLA_SCALE, accum_out=sb[:, 2:3]), waits=[nb_] + pfree, sig=True)
                sc_free[0] = [ex]
                ri = P.op("dve", lambda e, sb=sb: e.reciprocal(out=sb[:, 3:4], in_=sb[:, 2:3]), waits=[ex], sig=True)
                di, db, dfree = dg.next()
                dgv = P.op("dve", lambda e, sb=sb, db=db: e.tensor_scalar(out=db, in0=identb, scalar1=sb[:, 3:4], scalar2=0.0, op0=ALU.mult, op1=ALU.add), waits=[ri, c_ib] + dfree, sig=True)
                U[n].update({"ex": ex, "dg": dgv, "pi": pi_, "pb": pb_, "di": di, "db": db, "si": si})

            def stC(n, g=g, units=units, U=U):
                hh, i = units[n]
                u = U[n]
                pb_, db = u["pb"], u["db"]
                ti, tbuf, tfree = pTb.next()
                nblk = i + 1
                evs = []
                lastT = None
                for b0 in range(0, nblk, 4):
                    nb4 = min(4, nblk - b0)
                    bi, pt, ptfree = ps_t.next()
                    for q in range(nb4):
                        kb = b0 + q
                        lastT = P.op("pe", lambda e, pt=pt, q=q, kb=kb, pb_=pb_, db=db: e.matmul(pt[:, q * 128:(q + 1) * 128], lhsT=pb_[:, kb * 128:(kb + 1) * 128], rhs=db, start=True, stop=True),
                                     waits=([u["ex"], u["dg"]] + ptfree) if q == 0 else (), sig=(q == nb4 - 1))
                    ev = P.op("act", lambda e, pt=pt, b0=b0, nb4=nb4, tbuf=tbuf: e.activation(out=tbuf[:, b0:b0 + nb4, :].rearrange("p a b -> p (a b)"), in_=pt[:, 0:nb4 * 128], func=AF.Copy),
                              waits=[lastT] + (tfree if b0 == 0 else []), sig=True)
                    ps_t.release(bi, [ev])
                    evs.append(ev)
                Pb.release(u["pi"], [lastT])
                dg.release(u["di"], [lastT])
                sm.release(u["si"], [u["dg"]])
                oi, po, pofree = ps_o.next()
                for kb in range(nblk):
                    pv = P.op("pe", lambda e, po=po, kb=kb, hh=hh, tbuf=tbuf, nblk=nblk: e.matmul(po, lhsT=vt[:, kb, hh * 128:(hh + 1) * 128], rhs=tbuf[:, kb, :], start=(kb == 0), stop=(kb == nblk - 1)),
                              waits=(evs + pofree) if kb == 0 else (), sig=(kb == nblk - 1))
                pTb.release(ti, [pv])
                if i == 0:
                    ast_state["i"], ast_state["buf"], ast_state["free"] = ast.next()
                    ast_state["evs"] = []
                abuf = ast_state["buf"]
                oe = P.op("dve", lambda e, po=po, abuf=abuf, i=i: e.tensor_copy(out=abuf[:, i * 128:(i + 1) * 128], in_=po), waits=[pv] + (ast_state["free"] if i == 0 else []), sig=True)
                ps_o.release(oi, [oe])
                ast_state["evs"].append(oe)
                U[n]["pv"] = pv
                U[n]["oe"] = oe
                if i == NT - 1:
                    ai = ast_state["i"]
                    h = g * 4 + hh
                    st = P.dma("sp", s_ast[ai], lambda e, abuf=abuf, h=h: e.dma_start(out=attnT[h], in_=abuf), waits=ast_state["evs"])
                    ast.release(ai, [st])

            run_skewed(len(units), stA, stB, stC, single_buf=True)
            grp_free = [U[k][w] for k in range(len(units) - 4, len(units)) for w in ("pv", "oe")]
        P.barrier()
        if stop_after <= 3:
            P.run_block()
            return nc

        A.reset()
        qsT = A.alloc([16, T], BF16)
        kdup = A.alloc([4, T], BF16)
        vpd = A.alloc([NT, 1024], BF16)
        bm = A.alloc([32, 256], F32)
        mk = A.alloc([256], F32)
        snk = A.alloc([32], F32)
        ssb = Ring([A.alloc([256], F32) for _ in range(4)])
        Pb4 = Ring([A.alloc([256], BF16) for _ in range(4)])
        pT4 = Ring([A.alloc([2, 128], BF16) for _ in range(4)])
        dg4 = Ring([A.alloc([128], BF16) for _ in range(4)])
        sm4 = Ring([A.alloc([8], F32) for _ in range(6)])
        bst = Ring([A.alloc([T], BF16) for _ in range(2)])
        s_bst = [P.dma_sem(f"s_bst{i}") for i in range(2)]
        s_l4 = P.dma_sem("s_l4")
        L4 = [P.dma("sp", s_l4, lambda e: e.dma_start(out=qsT, in_=projT[12:28].rearrange("c p t -> p c t"))),
              P.dma("sp", s_l4, lambda e: e.dma_start(out=vpd, in_=vpad_d.rearrange("(tt p) f -> p tt f", p=128))),
              P.dma("sp", s_l4, lambda e: e.dma_start(out=bm, in_=biasg_d.rearrange("h q k -> q h k"))),
              P.dma("sp", s_l4, lambda e: e.dma_start(out=mk, in_=maskc_d)),
              P.dma("sp", s_l4, lambda e: e.dma_start(out=snk, in_=sinks_d.partition_broadcast(128)))]
        for gk in range(4):
            for half in range(2):
                L4.append(P.dma("sp", s_l4, lambda e, gk=gk, half=half: e.dma_start(out=kdup[half * 64:(half + 1) * 64, gk, :], in_=projT[28 + gk // 2, (gk % 2) * 64:(gk % 2) * 64 + 64, :])))
        P.group(L4)
        bme = None
        for h in range(32):
            bme = P.op("dve", lambda e, h=h: e.tensor_tensor(out=bm[:, h, :], in0=bm[:, h, :], in1=mk, op=ALU.add), waits=L4 if h == 0 else (), sig=(h == 31))
        ready4 = L4 + [bme]
        ps_s4 = Ring([ps[:, 0, 0:256], ps[:, 1, 0:256], ps[:, 2, 0:256]])
        ps_t4 = Ring([ps[:, 3, 0:256], ps[:, 4, 0:256]])
        ps_o4 = Ring([ps[:, 5, 0:128], ps[:, 6, 0:128]])
        units4 = [(j, nb, hx) for j in range(16) for nb in range(NT) for hx in range(2)]
        U4 = {}
        ost = {}

        def s4A(n):
            j, nb, hx = units4[n]
            h = 2 * j + hx
            gk = j // 4
            r0_ = hx * 64
            k0 = (nb - 1) * 128 if nb > 0 else 0
            nk = 256 if nb > 0 else 128
            bi, pss, pfree = ps_s4.next()
            mm = P.op("pe", lambda e: e.matmul(pss[:, 0:nk], lhsT=qsT[r0_:r0_ + 64, j, nb * 128:(nb + 1) * 128], rhs=kdup[r0_:r0_ + 64, gk, k0:k0 + nk], start=True, stop=True),
                      waits=ready4 + pfree, sig=True)
            U4[n] = {"S": mm, "bi": bi, "pss": pss, "nk": nk, "h": h, "gk": gk}

        def s4B(n):
            j, nb, hx = units4[n]
            u = U4[n]
            nk, h, pss = u["nk"], u["h"], u["pss"]
            c0 = 0 if nb > 0 else 128
            si, sb, sfree = ssb.next()
            mi, st_, mfree = sm4.next()
            a = P.op("dve", lambda e: e.scalar_tensor_tensor(out=sb[:, 0:nk], in0=pss[:, 0:nk], scalar=SWA_SCALE, in1=bm[:, h, c0:c0 + nk], op0=ALU.mult, op1=ALU.add), waits=[u["S"]] + sfree, sig=True)
            ps_s4.release(u["bi"], [a])
            mx = P.op("dve", lambda e: e.reduce_max(out=st_[:, 0:1], in_=sb[:, 0:nk], axis=AX.X), waits=[a] + mfree, sig=True)
            ng_ = P.op("dve", lambda e: e.tensor_scalar(out=st_[:, 1:2], in0=st_[:, 0:1], scalar1=snk[:, h:h + 1], scalar2=-1.0, op0=ALU.max, op1=ALU.mult), waits=[mx], sig=True)
            pi_, pb_, pfree = Pb4.next()
            ex = P.op("act", lambda e: e.activation(out=pb_[:, 0:nk], in_=sb[:, 0:nk], func=AF.Exp, bias=st_[:, 1:2], scale=1.0, accum_out=st_[:, 2:3]), waits=[ng_] + pfree, sig=True)
            es_ = P.op("act", lambda e: e.activation(out=st_[:, 3:4], in_=snk[:, h:h + 1], func=AF.Exp, bias=st_[:, 1:2], scale=1.0), waits=[ng_], sig=True)
            ssb.release(si, [ex])
            dn = P.op("dve", lambda e: e.tensor_tensor(out=st_[:, 4:5], in0=st_[:, 2:3], in1=st_[:, 3:4], op=ALU.add), waits=[ex, es_], sig=True)
            ri = P.op("dve", lambda e: e.reciprocal(out=st_[:, 5:6], in_=st_[:, 4:5]), waits=[dn], sig=True)
            di, db, dfree = dg4.next()
            dgv = P.op("dve", lambda e: e.tensor_scalar(out=db, in0=identb, scalar1=st_[:, 5:6], scalar2=0.0, op0=ALU.mult, op1=ALU.add), waits=[ri] + dfree, sig=True)
            sm4.release(mi, [dgv])
            u.update({"ex": ex, "dg": dgv, "pi": pi_, "pb": pb_, "di": di, "db": db})

        def s4C(n):
            j, nb, hx = units4[n]
            u = U4[n]
            nk, gk, pb_, db = u["nk"], u["gk"], u["pb"], u["db"]
            nblk = nk // 128
            bi, pt, ptfree = ps_t4.next()
            for kb in range(nblk):
                lastT = P.op("pe", lambda e, kb=kb: e.matmul(pt[:, kb * 128:(kb + 1) * 128], lhsT=pb_[:, kb * 128:(kb + 1) * 128], rhs=db, start=True, stop=True),
                             waits=([u["ex"], u["dg"]] + ptfree) if kb == 0 else (), sig=(kb == nblk - 1))
            Pb4.release(u["pi"], [lastT])
            dg4.release(u["di"], [lastT])
            ti, tbuf, tfree = pT4.next()
            ev = P.op("act", lambda e: e.activation(out=tbuf[:, 0:nblk, :].rearrange("p a b -> p (a b)"), in_=pt[:, 0:nk], func=AF.Copy), waits=[lastT] + tfree, sig=True)
            ps_t4.release(bi, [ev])
            if hx == 0:
                ost["oi"], ost["po"], ost["pofree"] = ps_o4.next()
            po = ost["po"]
            tt0 = nb - 1 if nb > 0 else 0
            for kb in range(nblk):
                pv = P.op("pe", lambda e, kb=kb: e.matmul(po, lhsT=vpd[:, tt0 + kb, (gk * 2 + hx) * 128:(gk * 2 + hx + 1) * 128], rhs=tbuf[:, kb, :], start=(hx == 0 and kb == 0), stop=(hx == 1 and kb == nblk - 1)),
                          waits=([ev] + (ost["pofree"] if hx == 0 else [])) if kb == 0 else (), sig=(kb == nblk - 1))
            pT4.release(ti, [pv])
            if hx == 1:
                if nb == 0:
                    ost["si"], ost["sbuf"], ost["sfree"] = bst.next()
                    ost["evs"] = []
                sbuf = ost["sbuf"]
                oe = P.op("dve", lambda e: e.tensor_copy(out=sbuf[:, nb * 128:(nb + 1) * 128], in_=po), waits=[pv] + (ost["sfree"] if nb == 0 else []), sig=True)
                ps_o4.release(ost["oi"], [oe])
                ost["evs"].append(oe)
                if nb == NT - 1:
                    si_ = ost["si"]
                    st = P.dma("sp", s_bst[si_], lambda e: e.dma_start(out=attnT[16 + j], in_=sbuf), waits=ost["evs"])
                    bst.release(si_, [st])

        run_skewed(len(units4), s4A, s4B, s4C)
        P.barrier()
        if stop_after <= 4:
            P.run_block()
            return nc

        A.reset()
        aT = A.alloc([16, T], BF16)
        bT = A.alloc([16, T], BF16)
        wa5 = Ring([A.alloc([16, 256], BF16) for _ in range(2)])
        wb5 = Ring([A.alloc([16, 256], BF16) for _ in range(2)])
        ga5 = Ring([A.alloc([T], BF16) for _ in range(2)])
        gb5 = Ring([A.alloc([T], BF16) for _ in range(2)])
        t15 = Ring([A.alloc([512], F32) for _ in range(2)])
        t25 = Ring([A.alloc([512], F32) for _ in range(2)])
        zst = Ring([A.alloc([T], BF16) for _ in range(2)])
        s_wa5 = [P.dma_sem(f"s_wa5{i}") for i in range(2)]
        s_wb5 = [P.dma_sem(f"s_wb5{i}") for i in range(2)]
        s_ga5 = [P.dma_sem(f"s_ga5{i}") for i in range(2)]
        s_gb5 = [P.dma_sem(f"s_gb5{i}") for i in range(2)]
        s_zst = [P.dma_sem(f"s_zst{i}") for i in range(2)]
        s_l5 = P.dma_sem("s_l5")
        L5 = [P.dma("sp", s_l5, lambda e: e.dma_start(out=aT, in_=attnT[0:16].rearrange("c p t -> p c t"))),
              P.dma("sp", s_l5, lambda e: e.dma_start(out=bT, in_=attnT[16:32].rearrange("c p t -> p c t")))]
        P.group(L5)
        ps_a5 = Ring([ps[:, 0, :], ps[:, 2, :], ps[:, 4, :], ps[:, 6, :]])
        ps_b5 = Ring([ps[:, 1, :], ps[:, 3, :], ps[:, 5, :], ps[:, 7, :]])
        for s in range(16):
            wai, wab, waf = wa5.next()
            wbi, wbb, wbf = wb5.next()
            lda = P.dma("pool", s_wa5[wai], lambda e, wab=wab, s=s: e.dma_start(out=wab, in_=wslab(w_pa, 0, 16, s * 256, (s + 1) * 256)), waits=waf)
            ldb = P.dma("pool", s_wb5[wbi], lambda e, wbb=wbb, s=s: e.dma_start(out=wbb, in_=wslab(w_pb, 0, 16, s * 256, (s + 1) * 256)), waits=wbf)
            for cc in range(2):
                n = s * 2 + cc
                gai, gab, gaf = ga5.next()
                gbi, gbb, gbf = gb5.next()
                lga = P.dma("sp", s_ga5[gai], lambda e, gab=gab, n=n: e.dma_start(out=gab, in_=gatesT[n]), waits=gaf)
                lgb = P.dma("sp", s_gb5[gbi], lambda e, gbb=gbb, n=n: e.dma_start(out=gbb, in_=gatesT[32 + n]), waits=gbf)
                zi, zb, zf = zst.next()
                zevs = []
                for tb in range(NTB):
                    ai, pa, paf = ps_a5.next()
                    for kc in range(16):
                        mma = P.op("pe", lambda e, pa=pa, wab=wab, cc=cc, kc=kc, tb=tb: e.matmul(pa, lhsT=wab[:, kc, cc * 128:(cc + 1) * 128], rhs=aT[:, kc, tb * 512:(tb + 1) * 512], start=(kc == 0), stop=(kc == 15)),
                                   waits=([lda] + L5 + paf) if kc == 0 else (), sig=(kc == 15))
                    bi_, pb5, pbf = ps_b5.next()
                    for kc in range(16):
                        mmb = P.op("pe", lambda e, pb5=pb5, wbb=wbb, cc=cc, kc=kc, tb=tb: e.matmul(pb5, lhsT=wbb[:, kc, cc * 128:(cc + 1) * 128], rhs=bT[:, kc, tb * 512:(tb + 1) * 512], start=(kc == 0), stop=(kc == 15)),
                                   waits=([ldb] + L5 + pbf) if kc == 0 else (), sig=(kc == 15))
                    i1, t1b, f1 = t15.next()
                    i2, t2b, f2 = t25.next()
                    a1 = P.op("dve", lambda e, t1b=t1b, pa=pa, gab=gab, tb=tb: e.tensor_tensor(out=t1b, in0=pa, in1=gab[:, tb * 512:(tb + 1) * 512], op=ALU.mult), waits=[mma, lga] + f1, sig=True)
                    a2 = P.op("dve", lambda e, t2b=t2b, pb5=pb5, gbb=gbb, tb=tb: e.tensor_tensor(out=t2b, in0=pb5, in1=gbb[:, tb * 512:(tb + 1) * 512], op=ALU.mult), waits=[mmb, lgb] + f2, sig=True)
                    a3 = P.op("dve", lambda e, t1b=t1b, t2b=t2b, zb=zb, tb=tb: e.tensor_tensor(out=zb[:, tb * 512:(tb + 1) * 512], in0=t1b, in1=t2b, op=ALU.add), waits=[a1, a2] + (zf if tb == 0 else []), sig=True)
                    ps_a5.release(ai, [a1])
                    ps_b5.release(bi_, [a2])
                    t15.release(i1, [a3])
                    t25.release(i2, [a3])
                    zevs.append(a3)
                ga5.release(gai, [a1])
                gb5.release(gbi, [a2])
                st = P.dma("sp", s_zst[zi], lambda e, zb=zb, n=n: e.dma_start(out=zT[n], in_=zb), waits=zevs)
                zst.release(zi, [st])
            wa5.release(wai, [mma])
            wb5.release(wbi, [mmb])
        P.barrier()
        if stop_after <= 5:
            P.run_block()
            return nc

        def build_bc(gvcol, gvbc, ready):
            dgf = Ring([A.alloc([128], F32) for _ in range(2)])
            ev = None
            for g4 in range(8):
                bi, pb, pfree = ps_ring8.next()
                for q in range(4):
                    c = g4 * 4 + q
                    di, db, dfree = dgf.next()
                    dv = P.op("dve", lambda e, db=db, c=c: e.tensor_scalar(out=db, in0=identf, scalar1=gvcol[:, c:c + 1], scalar2=0.0, op0=ALU.mult, op1=ALU.add), waits=ready + dfree + cload, sig=True)
                    mm = P.op("pe", lambda e, pb=pb, db=db, q=q: e.matmul(pb[:, q * 128:(q + 1) * 128], lhsT=onesf, rhs=db, start=True, stop=True), waits=[dv, c_of] + (pfree if q == 0 else []), sig=True)
                    dgf.release(di, [mm])
                ev = P.op("act", lambda e, pb=pb, g4=g4: e.activation(out=gvbc[:, g4 * 512:(g4 + 1) * 512], in_=pb, func=AF.Copy), waits=[mm], sig=True)
                ps_ring8.release(bi, [ev])
            return ev

        def gemm_a_acc(inT_d, KCtot, W, gvcol, s_pref):
            A.reset()
            acc = A.alloc([NT, 1024], F32)
            ug = Ring([A.alloc([8, T], BF16) for _ in range(2)])
            wg = Ring([A.alloc([8, 1024], BF16) for _ in range(2)])
            gvbc = A.alloc([D], F32)
            yst = Ring([A.alloc([1024], BF16) for _ in range(2)])
            jk = A.alloc([1024], BF16)
            s_ug = [P.dma_sem(f"{s_pref}_ug{i}") for i in range(2)]
            s_wg_ = [P.dma_sem(f"{s_pref}_wg{i}") for i in range(2)]
            s_yst = [P.dma_sem(f"{s_pref}_yst{i}") for i in range(2)]
            bc_ready = build_bc(gvcol, gvbc, mod_ready)
            NKG = KCtot // 8
            acc_free = [[] for _ in range(NT)]
            acc_last = [[None, None] for _ in range(NT)]
            jk_free = []
            for ng in range(4):
                for kg in range(NKG):
                    ui, ub, uf = ug.next()
                    wi_, wb_, wf = wg.next()
                    lu = P.dma("sp", s_ug[ui], lambda e, ub=ub, kg=kg: e.dma_start(out=ub, in_=inT_d[kg * 8:(kg + 1) * 8].rearrange("c p t -> p c t")), waits=uf)
                    lw = P.dma("pool", s_wg_[wi_], lambda e, wb_=wb_, kg=kg, ng=ng: e.dma_start(out=wb_, in_=wslab(W, kg * 8, (kg + 1) * 8, ng * 1024, (ng + 1) * 1024)), waits=wf)
                    for tt in range(NT):
                        for nn in range(2):
                            bi, pb, pfree = ps_ring8.next()
                            for c in range(8):
                                mm = P.op("pe", lambda e, pb=pb, ub=ub, wb_=wb_, c=c, tt=tt, nn=nn: e.matmul(pb, lhsT=ub[:, c, tt * 128:(tt + 1) * 128], rhs=wb_[:, c, nn * 512:(nn + 1) * 512], start=(c == 0), stop=(c == 7)),
                                          waits=([lu, lw] + pfree) if c == 0 else (), sig=(c == 7))
                            dst = acc[:, tt, nn * 512:(nn + 1) * 512]
                            if kg == 0:
                                ev = P.op("act", lambda e, dst=dst, pb=pb: e.activation(out=dst, in_=pb, func=AF.Copy), waits=[mm] + acc_free[tt], sig=True)
                            else:
                                ev = P.op("dve", lambda e, dst=dst, pb=pb: e.tensor_tensor(out=dst, in0=pb, in1=dst, op=ALU.add), waits=[mm, acc_last[tt][nn]], sig=True)
                            acc_last[tt][nn] = ev
                            ps_ring8.release(bi, [ev])
                    ug.release(ui, [mm])
                    wg.release(wi_, [mm])
                for tt in range(NT):
                    w_acc = [acc_last[tt][0], acc_last[tt][1]]
                    sq = P.op("act", lambda e, tt=tt, ng=ng: e.activation(out=jk, in_=acc[:, tt, :], func=AF.Square, accum_out=ssqp[:, tt, ng:ng + 1]), waits=w_acc + jk_free, sig=True)
                    jk_free = [sq]
                    yi, yb, yf = yst.next()
                    ml = P.op("dve", lambda e, tt=tt, ng=ng, yb=yb: e.tensor_tensor(out=yb, in0=acc[:, tt, :], in1=gvbc[:, ng * 1024:(ng + 1) * 1024], op=ALU.mult), waits=w_acc + [bc_ready] + yf, sig=True)
                    st = P.dma("sp", s_yst[yi], lambda e, tt=tt, ng=ng, yb=yb: e.dma_start(out=ytok[tt * 128:(tt + 1) * 128, ng * 1024:(ng + 1) * 1024], in_=yb), waits=[ml])
                    yst.release(yi, [st])
                    acc_free[tt] = [sq, ml]
            r1_ = P.op("dve", lambda e: e.reduce_sum(out=rstdA, in_=ssqp, axis=AX.X), waits=[sq], sig=True)
            r2_ = P.op("dve", lambda e: e.tensor_scalar(out=rstdA, in0=rstdA, scalar1=1.0 / D, scalar2=EPS, op0=ALU.mult, op1=ALU.add), waits=[r1_], sig=True)
            r3_ = P.op("act", lambda e: e.activation(out=rstdA, in_=rstdA, func=AF.Sqrt), waits=[r2_], sig=True)
            r4_ = P.op("dve", lambda e: e.reciprocal(out=rstdA, in_=rstdA), waits=[r3_], sig=True)
            P.barrier()

        gemm_a_acc(zT, 32, w_out, gv1, "p6")
        if stop_after <= 6:
            P.run_block()
            return nc

        A.reset()
        hT = A.alloc([32, T], BF16)
        off_after_hT = A.off
        xts = Ring([A.alloc([D], F32) for _ in range(2)])
        yts = Ring([A.alloc([D], BF16) for _ in range(2)])
        xh = A.alloc([D], F32)
        s_x7 = [P.dma_sem(f"s_x7{i}") for i in range(2)]
        s_y7 = [P.dma_sem(f"s_y7{i}") for i in range(2)]
        s_o7 = [P.dma_sem(f"s_o7{i}") for i in range(2)]
        xh_free = []
        for tt in range(NT):
            xi, xt, xfree = xts.next()
            yi, yt, yfree = yts.next()
            lx = P.dma("sp", s_x7[xi], lambda e, xt=xt, tt=tt: e.dma_start(out=xt, in_=x_d[tt * 128:(tt + 1) * 128, :]), waits=xfree)
            ly = P.dma("sp", s_y7[yi], lambda e, yt=yt, tt=tt: e.dma_start(out=yt, in_=ytok[tt * 128:(tt + 1) * 128, :]), waits=yfree)
            x1e = P.op("dve", lambda e, xt=xt, yt=yt, tt=tt: e.scalar_tensor_tensor(out=xt, in0=yt, scalar=rstdA[:, tt:tt + 1], in1=xt, op0=ALU.mult, op1=ALU.add), waits=[lx, ly], sig=True)
            so = P.dma("sp", s_o7[xi], lambda e, xt=xt, tt=tt: e.dma_start(out=out_d[tt * 128:(tt + 1) * 128, :], in_=xt), waits=[x1e])
            rd, xhf, ev = norm_transpose(tt, xt, [x1e], yt, [x1e], xh, xh_free, gs2, sh2, hT)
            xts.release(xi, rd + [so])
            yts.release(yi, [rd[0]])
            xh_free = xhf
        P.barrier()
        if stop_after <= 7:
            P.run_block()
            return nc

        A.reset(off_after_hT)
        w8 = Ring([A.alloc([32, 256], BF16) for _ in range(2)])
        s_w8 = [P.dma_sem(f"s_w8_{i}") for i in range(2)]
        ust = Ring([A.alloc([T], BF16) for _ in range(3)])
        s_ust = [P.dma_sem(f"s_ust{i}") for i in range(3)]
        rl = Ring([A.alloc([512], F32) for _ in range(3)])
        cur8 = {}

        def epi8(ci, tb, pb, mm):
            if tb == 0:
                cur8["i"], cur8["buf"], cur8["free"] = ust.next()
                cur8["evs"] = []
            buf = cur8["buf"]
            ri, rb, rf = rl.next()
            ev = P.op("act", lambda e: e.activation(out=rb, in_=pb, func=AF.Relu), waits=[mm] + rf, sig=True)
            e2 = P.op("dve", lambda e: e.tensor_tensor(out=buf[:, tb * 512:(tb + 1) * 512], in0=rb, in1=rb, op=ALU.mult), waits=[ev] + (cur8["free"] if tb == 0 else []), sig=True)
            rl.release(ri, [e2])
            cur8["evs"].append(e2)
            if tb == NTB - 1:
                i = cur8["i"]
                st = P.dma("sp", s_ust[i], lambda e: e.dma_start(out=uT[ci], in_=buf), waits=cur8["evs"])
                ust.release(i, [st])
            return ev

        gemm_b("w_up", hT, 32, w_up, 0, 128, w8, s_w8, epi8)
        P.barrier()
        if stop_after <= 8:
            P.run_block()
            return nc

        gemm_a_acc(uT, 128, w_dn, gv2, "p9")

        A.reset()
        xts = Ring([A.alloc([D], F32) for _ in range(3)])
        yts = Ring([A.alloc([D], BF16) for _ in range(3)])
        s_xa = [P.dma_sem(f"s_xa{i}") for i in range(3)]
        s_ya = [P.dma_sem(f"s_ya{i}") for i in range(3)]
        s_oa = [P.dma_sem(f"s_oa{i}") for i in range(3)]
        for tt in range(NT):
            xi, xt, xfree = xts.next()
            yi, yt, yfree = yts.next()
            lx = P.dma("sp", s_xa[xi], lambda e, xt=xt, tt=tt: e.dma_start(out=xt, in_=out_d[tt * 128:(tt + 1) * 128, :]), waits=xfree)
            ly = P.dma("sp", s_ya[yi], lambda e, yt=yt, tt=tt: e.dma_start(out=yt, in_=ytok[tt * 128:(tt + 1) * 128, :]), waits=yfree)
            fe = P.op("dve", lambda e, xt=xt, yt=yt, tt=tt: e.scalar_tensor_tensor(out=xt, in0=yt, scalar=rstdA[:, tt:tt + 1], in1=xt, op0=ALU.mult, op1=ALU.add), waits=[lx, ly], sig=True)
            so = P.dma("sp", s_oa[xi], lambda e, xt=xt, tt=tt: e.dma_start(out=out_d[tt * 128:(tt + 1) * 128, :], in_=xt), waits=[fe])
            xts.release(xi, [so])
            yts.release(yi, [fe])
        P.barrier()
        P.run_block()
    return nc


def _t5_bucket_np(rel):
    nb = 16
    max_exact = 8
    ret = np.where(rel > 0, nb, 0)
    n = np.abs(rel)
    nf = np.maximum(n, 1).astype(np.float32)
    large = max_exact + (np.log(nf / max_exact) / np.log(128 / max_exact) * (nb - max_exact)).astype(np.int32)
    large = np.minimum(large, nb - 1)
    return ret + np.where(n < max_exact, n, large)


def _host_prep(inp):
    f32 = np.float32
    shared = {}
    w_in = inp["w_in"][0]
    kr = w_in[:, 1536:1600]
    shared["w_in_p"] = np.ascontiguousarray(np.concatenate(
        [w_in[:, 0:1024], w_in[:, 1024:1536], w_in[:, 1600:3648], w_in[:, 3648:3904], w_in[:, 3904:4160],
         kr, kr[:, 32:64], kr[:, 0:32]], axis=1), dtype=f32)
    w_uq = inp["w_uq"][0].reshape(1024, 16, 192)
    nope = w_uq[:, :, 0:128].reshape(1024, 2048)
    rope = w_uq[:, :, 128:192]
    rope_sw = np.concatenate([rope[:, :, 32:64], rope[:, :, 0:32]], axis=2)
    shared["w_uq_p"] = np.ascontiguousarray(np.concatenate([nope, rope.reshape(1024, 1024), rope_sw.reshape(1024, 1024)], axis=1), dtype=f32)
    w_ukv = inp["w_ukv"][0].reshape(512, 16, 256)
    shared["w_ukv_p"] = np.ascontiguousarray(np.concatenate([w_ukv[:, :, 0:128].reshape(512, 2048), w_ukv[:, :, 128:256].reshape(512, 2048)], axis=1), dtype=f32)
    for k in ("w_ada", "w_gate", "w_proj_a", "w_proj_b", "w_out", "w_ff_up", "w_ff_down"):
        shared[k] = np.ascontiguousarray(inp[k][0], dtype=f32)

    def col(v):
        return np.asarray(v, dtype=f32).reshape(-1, 128).T

    inv = (np.float32(10000.0) ** (-np.arange(32, dtype=np.float32) * np.float32(2.0 / 64))).astype(f32)
    colv = np.concatenate([col(inp["pre_norm_g"][0, 0]), col(inp["pre_norm_g"][0, 1]), col(inp["post_norm_g"][0, 0]), col(inp["post_norm_g"][0, 1]),
                           col(inp["q_norm_g"][0]), col(inp["kv_norm_g"][0]), col(inp["b_ada"][0]),
                           np.tile(inv, 4).reshape(128, 1)], axis=1)
    assert colv.shape == (128, CV_N)
    shared["colv"] = np.ascontiguousarray(colv, dtype=f32)
    shared["sinks"] = np.ascontiguousarray(inp["swa_sinks"][0].reshape(1, 32), dtype=f32)
    iq = np.arange(128)
    ik = np.arange(256) - 128
    rel = ik[None, :] - iq[:, None]
    bidx = _t5_bucket_np(rel)
    shared["biasg"] = np.ascontiguousarray(np.transpose(np.asarray(inp["rel_bias"], dtype=f32)[bidx], (2, 0, 1)))
    qh = iq[:, None] // 64
    kh = np.arange(256)[None, :] // 64
    valid = (kh >= qh) & (kh <= qh + 2)
    shared["maskc"] = np.where(valid, 0.0, NEG).astype(f32)
    shared["identf"] = np.eye(128, dtype=f32)
    mrow = np.zeros((1, 256), dtype=f32)
    mrow[0, 0:64] = 1.0
    mrow[0, 128 + 64:256] = -30000.0
    shared["mrow"] = mrow
    in_maps = []
    for b in range(8):
        m = dict(shared)
        m["x"] = np.ascontiguousarray(inp["x"][b], dtype=f32)
        m["cT"] = np.ascontiguousarray(col(inp["c"][b]))
        m["pos"] = np.ascontiguousarray(inp["positions"][b].reshape(1, T).astype(np.int32))
        in_maps.append(m)
    return in_maps


_NC_CACHE = {}


def kernel(**inputs):
    inp = {k: np.asarray(v) for k, v in inputs.items()}
    in_maps = _host_prep(inp)
    if "nc" not in _NC_CACHE:
        _NC_CACHE["nc"] = build_nc()
    res = run_bass_kernel_spmd(_NC_CACHE["nc"], in_maps, core_ids=list(range(8)))
    out = np.stack([np.asarray(r["out"], dtype=np.float32) for r in res.results], axis=0)
    return out
```

```python
import numpy as np
import ml_dtypes
from contextlib import ExitStack
import concourse.bass as bass
import concourse.mybir as mybir
from concourse.bass_utils import run_bass_kernel_spmd

F32, BF16, I32 = mybir.dt.float32, mybir.dt.bfloat16, mybir.dt.int32
AF = mybir.ActivationFunctionType
ALU = mybir.AluOpType
AX = mybir.AxisListType
ENG = ["pe", "act", "dve", "pool", "sp"]

D = 4096
T = 2048
NT = T // 128
NTB = T // 512
KC_D = D // 128
DFF = 4 * D
EPS = 1e-6
NEG = -1e30
MLA_HEADS = 16
SWA_HEADS = 32
MLA_SCALE = float((128 + 64) ** -0.5)
SWA_SCALE = float(64 ** -0.5)

CV_PRE0, CV_PRE1, CV_POST0, CV_POST1, CV_QG, CV_KVG, CV_BADA, CV_INV, CV_N = 0, 32, 64, 96, 128, 136, 140, 332, 333


class Sem:
    def __init__(self, h):
        self.h = h
        self.count = 0


class Rec:
    __slots__ = ("fn", "waits", "sig", "val", "amount")

    def __init__(self, fn, waits):
        self.fn = fn
        self.waits = [w for w in waits if w is not None]
        self.sig = None
        self.val = 0
        self.amount = 0


class Prog:
    def __init__(self, nc, es):
        self.nc = nc
        self.es = es
        self.q = {e: [] for e in ENG}
        self.psem = {e: self.new_sem("p_" + e) for e in ENG}
        self.dsems = []

    def new_sem(self, name):
        return Sem(self.es.enter_context(self.nc.semaphore(name)))

    def dma_sem(self, name):
        s = self.new_sem(name)
        self.dsems.append(s)
        return s

    def op(self, eng, fn, waits=(), sig=False):
        rec = Rec(fn, waits)
        if sig:
            s = self.psem[eng]
            s.count += 1
            rec.sig, rec.val, rec.amount = s, s.count, 1
        self.q[eng].append(rec)
        return rec

    def dma(self, eng, sem, fn, waits=()):
        rec = Rec(fn, waits)
        sem.count += 16
        rec.sig, rec.val, rec.amount = sem, sem.count, 16
        self.q[eng].append(rec)
        return rec

    def group(self, recs):
        m = max(r.val for r in recs)
        for r in recs:
            r.val = m
        return recs

    def barrier(self):
        class _W:
            pass
        dwaits = []
        for s in self.dsems:
            if s.count > 0:
                w = _W()
                w.sig, w.val = s, s.count
                dwaits.append(w)
        evs = []
        for e in ENG:
            if e in ("sp", "pool"):
                evs.append(self.op(e, lambda eng: eng.nop(), waits=dwaits, sig=True))
            else:
                q = self.q[e]
                if q and q[-1].sig is None:
                    s = self.psem[e]
                    s.count += 1
                    q[-1].sig, q[-1].val, q[-1].amount = s, s.count, 1
                if q:
                    evs.append(q[-1])
        for e in ENG:
            self.op(e, lambda eng: eng.nop(), waits=evs)
        return evs

    def replay(self, name, eng):
        waited = {}
        for rec in self.q[name]:
            for w in rec.waits:
                s, v = w.sig, w.val
                assert s is not None, "waiting on unsignaled op"
                if waited.get(id(s), 0) < v:
                    eng.wait_ge(s.h, v)
                    waited[id(s)] = v
            ins = rec.fn(eng)
            if rec.sig is not None:
                ins.then_inc(rec.sig.h, rec.amount)

    def run_block(self):
        nc = self.nc
        with nc.Block() as block:
            @block.tensor
            def _(e):
                self.replay("pe", e)

            @block.scalar
            def _(e):
                self.replay("act", e)

            @block.vector
            def _(e):
                self.replay("dve", e)

            @block.gpsimd
            def _(e):
                self.replay("pool", e)

            @block.sync
            def _(e):
                self.replay("sp", e)


DTSZ = {F32: 4, BF16: 2, I32: 4}


class Arena:
    def __init__(self, nc, name, nbytes):
        self.t = nc.alloc_sbuf_tensor(name, [128, nbytes // 2], BF16)
        self.nbytes = nbytes
        self.off = 0

    def reset(self, off=0):
        self.off = off

    def alloc(self, free_shape, dtype, parts=128):
        n = int(np.prod(free_shape))
        sz = n * DTSZ[dtype]
        sz_al = (sz + 63) // 64 * 64
        assert self.off + sz_al <= self.nbytes, f"arena overflow {self.off}+{sz_al}>{self.nbytes}"
        v = self.t[0:parts, self.off // 2:(self.off + sz) // 2]
        self.off += sz_al
        if dtype != BF16:
            v = v.bitcast(dtype)
        if len(free_shape) == 1:
            return v
        names = [f"d{i}" for i in range(len(free_shape))]
        pat = "p (" + " ".join(names) + ") -> p " + " ".join(names)
        kw = {nm: int(s) for nm, s in zip(names[1:], free_shape[1:])}
        return v.rearrange(pat, **kw)


class Ring:
    def __init__(self, bufs):
        self.bufs = bufs
        self.free = [[] for _ in bufs]
        self.i = 0

    def next(self):
        idx = self.i % len(self.bufs)
        self.i += 1
        return idx, self.bufs[idx], list(self.free[idx])

    def release(self, idx, evs):
        self.free[idx] = list(evs)


def wslab(W, c0, c1, n0, n1):
    return W.rearrange("(c p) n -> p c n", p=128)[:, c0:c1, n0:n1]


def build_nc(stop_after=99, debug=False):
    nc = bass.Bass("TRN2", target_bir_lowering=False)
    dk = "ExternalOutput" if debug else "Internal"

    def din(name, shape, dt):
        return nc.dram_tensor(name, shape, dt, kind="ExternalInput").ap()

    x_d = din("x", [T, D], F32)
    cT_d = din("cT", [128, 32], F32)
    pos_d = din("pos", [1, T], I32)
    colv_d = din("colv", [128, CV_N], F32)
    sinks_d = din("sinks", [1, 32], F32)
    biasg_d = din("biasg", [32, 128, 256], F32)
    maskc_d = din("maskc", [128, 256], F32)
    identf_d = din("identf", [128, 128], F32)
    mrow_d = din("mrow", [1, 256], F32)
    w_ada = din("w_ada", [D, 6 * D], F32)
    w_in = din("w_in_p", [D, 4224], F32)
    w_gate = din("w_gate", [D, 2 * D], F32)
    w_uq = din("w_uq_p", [1024, 4096], F32)
    w_ukv = din("w_ukv_p", [512, 4096], F32)
    w_pa = din("w_proj_a", [2048, D], F32)
    w_pb = din("w_proj_b", [2048, D], F32)
    w_out = din("w_out", [D, D], F32)
    w_up = din("w_ff_up", [D, DFF] if stop_after >= 8 else [128, 128], F32)
    w_dn = din("w_ff_down", [DFF, D] if stop_after >= 8 else [128, 128], F32)
    out_d = nc.dram_tensor("out", [T, D], F32, kind="ExternalOutput").ap()

    projT = nc.dram_tensor("projT", [33, 128, T], BF16, kind=dk).ap()
    vpad_d = nc.dram_tensor("vpad", [T, 1024], BF16, kind=dk).ap()
    gatesT = nc.dram_tensor("gatesT", [64, 128, T], BF16, kind=dk).ap()
    attnT = nc.dram_tensor("attnT", [32, 128, T], BF16, kind=dk).ap()
    zT = nc.dram_tensor("zT", [32, 128, T], BF16, kind=dk).ap()
    ytok = nc.dram_tensor("ytok", [T, D], BF16, kind=dk).ap()
    uT = nc.dram_tensor("uT", [128, 128, T], BF16, kind="Internal").ap()
    dbg_d = nc.dram_tensor("dbg", [128, 512], F32, kind=dk).ap()

    with ExitStack() as es:
        P = Prog(nc, es)
        ps = es.enter_context(nc.psum_tensor("ps", [128, 8, 512], F32))
        pers = Arena(nc, "pers", 6 * 1024)
        A = Arena(nc, "arena", 200 * 1024)

        colv = pers.alloc([CV_N], F32)
        modc = pers.alloc([192], F32)
        gs1 = pers.alloc([32], F32)
        gs2 = pers.alloc([32], F32)
        gv1 = pers.alloc([32], F32)
        gv2 = pers.alloc([32], F32)
        ssqp = pers.alloc([NT, 4], F32)
        rstdA = pers.alloc([NT], F32)
        stat = pers.alloc([NT, 4], F32)
        identf = pers.alloc([128], F32)
        identb = pers.alloc([128], BF16)
        onesf = pers.alloc([128], F32)
        onesb = pers.alloc([128], BF16)
        mrowb = pers.alloc([256], BF16, parts=1)
        sh1 = modc[:, 0:32]
        sc1 = modc[:, 32:64]
        g1c = modc[:, 64:96]
        sh2 = modc[:, 96:128]
        sc2 = modc[:, 128:160]
        g2c = modc[:, 160:192]

        s_const = P.dma_sem("s_const")
        cload = [
            P.dma("sp", s_const, lambda e: e.dma_start(out=colv, in_=colv_d)),
            P.dma("sp", s_const, lambda e: e.dma_start(out=identf, in_=identf_d)),
        ]
        P.group(cload)
        c_ib = P.op("dve", lambda e: e.tensor_copy(out=identb, in_=identf), waits=cload, sig=True)
        c_of = P.op("dve", lambda e: e.memset(onesf, 1.0), sig=True)
        c_ob = P.op("dve", lambda e: e.memset(onesb, 1.0), sig=True)
        consts_ready = cload + [c_ib, c_of, c_ob]

        A.reset()
        cTf = A.alloc([32], F32)
        cTb = A.alloc([32], BF16)
        mrowf = A.alloc([256], F32, parts=1)
        rowsb = Ring([A.alloc([512], F32, parts=1) for _ in range(2)])
        wada = Ring([A.alloc([32, 512], BF16) for _ in range(3)])
        s_c = P.dma_sem("s_c")
        l_c = P.dma("sp", s_c, lambda e: e.dma_start(out=cTf, in_=cT_d))
        l_m = P.dma("sp", s_c, lambda e: e.dma_start(out=mrowf, in_=mrow_d))
        P.group([l_c, l_m])
        c_silu = P.op("act", lambda e: e.activation(out=cTb, in_=cTf, func=AF.Silu), waits=[l_c], sig=True)
        c_mrow = P.op("dve", lambda e: e.tensor_copy(out=mrowb, in_=mrowf), waits=[l_m], sig=True)
        s_wada = [P.dma_sem(f"s_wada{i}") for i in range(3)]
        ps_row = Ring([ps[0:1, 0, :], ps[0:1, 1, :]])
        ps_col = Ring([ps[:, 2, 0:4], ps[:, 3, 0:4]])
        NJ = 48
        for j in range(NJ):
            wi, wb, wfree = wada.next()
            ld = P.dma("pool", s_wada[wi], lambda e, wb=wb, j=j: e.dma_start(out=wb, in_=wslab(w_ada, 0, 32, j * 512, (j + 1) * 512)), waits=wfree)
            ri, prow, prfree = ps_row.next()
            for kc in range(32):
                mm = P.op("pe", lambda e, prow=prow, wb=wb, kc=kc: e.matmul(prow, lhsT=cTb[:, kc:kc + 1], rhs=wb[:, kc, :], start=(kc == 0), stop=(kc == 31)),
                          waits=([ld, c_silu] + prfree) if kc == 0 else (), sig=(kc == 31))
            wada.release(wi, [mm])
            si, rsb, rsfree = rowsb.next()
            cp = P.op("act", lambda e, rsb=rsb, prow=prow: e.activation(out=rsb, in_=prow, func=AF.Copy), waits=[mm] + rsfree, sig=True)
            ps_row.release(ri, [cp])
            ci, pcol, pcfree = ps_col.next()
            for i in range(4):
                k1 = P.op("pe", lambda e, pcol=pcol, rsb=rsb, i=i: e.matmul(pcol[:, i:i + 1], lhsT=rsb[0:1, i * 128:(i + 1) * 128], rhs=onesf[0:1, 0:1], start=True, stop=True),
                          waits=([cp, c_of] + pcfree) if i == 0 else (), sig=(i == 3))
            rowsb.release(si, [k1])
            ad = P.op("dve", lambda e, pcol=pcol, j=j: e.tensor_tensor(out=modc[:, j * 4:(j + 1) * 4], in0=pcol, in1=colv[:, CV_BADA + j * 4:CV_BADA + (j + 1) * 4], op=ALU.add),
                      waits=[k1] + cload, sig=True)
            ps_col.release(ci, [ad])
        m1 = P.op("dve", lambda e: e.scalar_tensor_tensor(out=gs1, in0=sc1, scalar=1.0, in1=colv[:, CV_PRE0:CV_PRE0 + 32], op0=ALU.add, op1=ALU.mult), waits=[ad], sig=True)
        m2 = P.op("dve", lambda e: e.scalar_tensor_tensor(out=gs2, in0=sc2, scalar=1.0, in1=colv[:, CV_PRE1:CV_PRE1 + 32], op0=ALU.add, op1=ALU.mult), waits=[ad], sig=True)
        m3 = P.op("dve", lambda e: e.tensor_tensor(out=gv1, in0=g1c, in1=colv[:, CV_POST0:CV_POST0 + 32], op=ALU.mult), waits=[ad], sig=True)
        m4 = P.op("dve", lambda e: e.tensor_tensor(out=gv2, in0=g2c, in1=colv[:, CV_POST1:CV_POST1 + 32], op=ALU.mult), waits=[ad], sig=True)
        mod_ready = [m1, m2, m3, m4, c_mrow]
        P.barrier()
        if debug:
            s_dbg = P.dma_sem("s_dbg")
            P.dma("sp", s_dbg, lambda e: e.dma_start(out=dbg_d[:, 0:192], in_=modc))
            P.dma("sp", s_dbg, lambda e: e.dma_start(out=dbg_d[:, 192:224], in_=gs1))

        pst_ring = Ring([ps[:, b, :] for b in range(8)])

        def norm_transpose(tt, xt, xt_ready, junk, junk_free, xh, xh_free, gs, sh, hT):
            sq = P.op("act", lambda e: e.activation(out=junk, in_=xt, func=AF.Square, accum_out=stat[:, tt, 0:1]), waits=xt_ready + junk_free, sig=True)
            t1 = P.op("dve", lambda e: e.tensor_scalar(out=stat[:, tt, 1:2], in0=stat[:, tt, 0:1], scalar1=1.0 / D, scalar2=EPS, op0=ALU.mult, op1=ALU.add), waits=[sq], sig=True)
            t2 = P.op("act", lambda e: e.activation(out=stat[:, tt, 2:3], in_=stat[:, tt, 1:2], func=AF.Sqrt), waits=[t1], sig=True)
            t3 = P.op("dve", lambda e: e.reciprocal(out=stat[:, tt, 3:4], in_=stat[:, tt, 2:3]), waits=[t2], sig=True)
            sc = P.op("dve", lambda e: e.tensor_scalar(out=xh, in0=xt, scalar1=stat[:, tt, 3:4], scalar2=0.0, op0=ALU.mult, op1=ALU.add), waits=[t3] + xt_ready + xh_free, sig=True)
            last_tr = None
            for g in range(8):
                bi, pb, pfree = pst_ring.next()
                for q in range(4):
                    c = g * 4 + q
                    last_tr = P.op("pe", lambda e, pb=pb, q=q, c=c: e.transpose(pb[:, q * 128:(q + 1) * 128], xh[:, c * 128:(c + 1) * 128], identf),
                                   waits=([sc] + pfree + cload) if q == 0 else (), sig=(q == 3))
                for q in range(4):
                    c = g * 4 + q
                    ev = P.op("act", lambda e, pb=pb, q=q, c=c: e.activation(out=hT[:, c, tt * 128:(tt + 1) * 128], in_=pb[:, q * 128:(q + 1) * 128], func=AF.Identity,
                                                                            scale=gs[:, c:c + 1], bias=sh[:, c:c + 1]),
                              waits=([last_tr] + mod_ready) if q == 0 else (), sig=(q == 3))
                pst_ring.release(bi, [ev])
            return [sq, sc], [last_tr], ev

        A.reset()
        hT = A.alloc([32, T], BF16)
        off_after_hT = A.off
        xts = Ring([A.alloc([D], F32) for _ in range(2)])
        xh = A.alloc([D], F32)
        junk = A.alloc([D], BF16)
        s_x = [P.dma_sem(f"s_x{i}") for i in range(2)]
        xh_free, junk_free = [], []
        h_ready = []
        for tt in range(NT):
            xi, xt, xfree = xts.next()
            ld = P.dma("sp", s_x[xi], lambda e, xt=xt, tt=tt: e.dma_start(out=xt, in_=x_d[tt * 128:(tt + 1) * 128, :]), waits=xfree)
            rd, xhf, ev = norm_transpose(tt, xt, [ld], junk, junk_free, xh, xh_free, gs1, sh1, hT)
            xts.release(xi, rd)
            xh_free, junk_free = xhf, [rd[0]]
            h_ready = [ev]
        P.barrier()
        if stop_after <= 1:
            if debug:
                P.dma("sp", s_dbg, lambda e: e.dma_start(out=projT[0:32].rearrange("c p t -> p c t"), in_=hT))
            P.barrier()
            P.run_block()
            return nc

        ps_ring8 = Ring([ps[:, b, :] for b in range(8)])

        def gemm_b(name, inT, KC, W, col0, nchunks, wring, wsems, epi, rows_last=128, slab=2):
            nslab = (nchunks + slab - 1) // slab
            for s in range(nslab):
                c_lo = s * slab
                c_hi = min(nchunks, c_lo + slab)
                ncol = (c_hi - c_lo) * 128
                wi, wb, wfree = wring.next()
                ld = P.dma("pool", wsems[wi], lambda e, wb=wb, c_lo=c_lo, ncol=ncol: e.dma_start(out=wb[:, :, 0:ncol], in_=wslab(W, 0, KC, col0 + c_lo * 128, col0 + c_lo * 128 + ncol)), waits=wfree)
                mm = None
                for ci in range(c_lo, c_hi):
                    M = rows_last if ci == nchunks - 1 else 128
                    lo = (ci - c_lo) * 128
                    for tb in range(NTB):
                        bi, pb, pfree = ps_ring8.next()
                        for kc in range(KC):
                            mm = P.op("pe", lambda e, pb=pb, wb=wb, lo=lo, M=M, kc=kc, tb=tb: e.matmul(pb[0:M, :], lhsT=wb[:, kc, lo:lo + M], rhs=inT[:, kc, tb * 512:(tb + 1) * 512], start=(kc == 0), stop=(kc == KC - 1)),
                                      waits=([ld] + pfree) if kc == 0 else (), sig=(kc == KC - 1))
                        fr = epi(ci, tb, pb, mm)
                        ps_ring8.release(bi, [fr])
                wring.release(wi, [mm])

        A.reset(off_after_hT)
        w2 = Ring([A.alloc([32, 256], BF16) for _ in range(2)])
        s_w2 = [P.dma_sem(f"s_w2_{i}") for i in range(2)]
        stg = Ring([A.alloc([T], BF16) for _ in range(3)])
        s_st = [P.dma_sem(f"s_st{i}") for i in range(3)]
        vst = Ring([A.alloc([1024], BF16) for _ in range(2)])
        s_vst = [P.dma_sem(f"s_vst{i}") for i in range(2)]
        cur = {}

        def make_epi(dst, func):
            def epi(ci, tb, pb, mm):
                if tb == 0:
                    cur["i"], cur["buf"], cur["free"] = stg.next()
                    cur["evs"] = []
                buf = cur["buf"]
                w = [mm] + (cur["free"] if tb == 0 else [])
                if func is None:
                    ev = P.op("dve", lambda e: e.tensor_copy(out=buf[:, tb * 512:(tb + 1) * 512], in_=pb), waits=w, sig=True)
                else:
                    ev = P.op("act", lambda e: e.activation(out=buf[:, tb * 512:(tb + 1) * 512], in_=pb, func=func), waits=w, sig=True)
                cur["evs"].append(ev)
                if tb == NTB - 1:
                    i = cur["i"]
                    st = P.dma("sp", s_st[i], lambda e: e.dma_start(out=dst[ci], in_=buf), waits=cur["evs"])
                    stg.release(i, [st])
                return ev
            return epi

        import os as _os
        _p2n = int(_os.environ.get("K_P2N", "30"))
        _p2rest = _os.environ.get("K_P2REST", "1") == "1"
        epi_in = make_epi(projT, None)
        gemm_b("w_in_a", hT, 32, w_in, 0, _p2n, w2, s_w2, epi_in)
        if _os.environ.get("K_P2B", "1") == "1":
            gemm_b("w_in_b", hT, 32, w_in, 32 * 128, 1, w2, s_w2, lambda ci, tb, pb, mm: epi_in(32, tb, pb, mm))
        _v_on = _os.environ.get("K_P2V", "1") == "1"
        if not _v_on:
            stg_save = None
        wi, wb, wfree = w2.next() if _v_on else (0, w2.bufs[0], [])
        ldv = None if not _v_on else P.dma("pool", s_w2[wi], lambda e, wb=wb: e.dma_start(out=wb, in_=wslab(w_in, 0, 32, 30 * 128, 32 * 128)), waits=wfree)
        vz = [P.op("dve", lambda e, b=b: e.memset(b, 0.0), sig=True) for b in vst.bufs] if _v_on else []
        for tt in range(NT if _os.environ.get("K_P2V", "1") == "1" else 0):
            bi, pb, pfree = ps_ring8.next()
            for kc in range(32):
                mm = P.op("pe", lambda e, pb=pb, kc=kc, tt=tt, wb=wb, hT=hT: e.matmul(pb[:, 0:256], lhsT=hT[:, kc, tt * 128:(tt + 1) * 128], rhs=wb[:, kc, :], start=(kc == 0), stop=(kc == 31)),
                          waits=([ldv] + pfree) if kc == 0 else (), sig=(kc == 31))
            vi, vb, vfree = vst.next()
            vb4 = vb.rearrange("p (g r d) -> p g r d", g=4, r=2)
            e0 = P.op("dve", lambda e, vb4=vb4, pb=pb: e.tensor_copy(out=vb4[:, :, 0, 0:64], in_=pb[:, 0:256].rearrange("p (g d) -> p g d", g=4)), waits=[mm] + vfree + vz, sig=True)
            e1 = P.op("dve", lambda e, vb4=vb4, pb=pb: e.tensor_copy(out=vb4[:, :, 1, 64:128], in_=pb[:, 0:256].rearrange("p (g d) -> p g d", g=4)), waits=[e0], sig=True)
            ps_ring8.release(bi, [e0, e1])
            st = P.dma("sp", s_vst[vi], lambda e, vb=vb, tt=tt: e.dma_start(out=vpad_d[tt * 128:(tt + 1) * 128, :], in_=vb), waits=[e0, e1])
            vst.release(vi, [st])
        if _v_on:
            w2.release(wi, [mm])
        gemm_b("w_gate", hT, 32, w_gate, 0, int(_os.environ.get("K_P2G", "64")), w2, s_w2, make_epi(gatesT, {"none": None, "copy": AF.Copy, "sigmoid": AF.Sigmoid}[_os.environ.get("K_GATEF", "sigmoid")]))
        P.barrier()
        if stop_after <= 2:
            P.run_block()
            return nc

        def run_skewed(n_units, stageA, stageB, stageC, single_buf=False, stageB2=None, stageC2=None):
            if single_buf:
                for n in range(n_units + 1):
                    if n < n_units:
                        stageA(n)
                    if 0 <= n - 1 < n_units:
                        stageC(n - 1)
                    if n < n_units:
                        stageB(n)
                    if 0 <= n - 1 < n_units:
                        stageC2(n - 1)
                return
            for n in range(n_units + 3):
                if n < n_units:
                    stageA(n)
                if 0 <= n - 1 < n_units:
                    stageB(n - 1)
                if 0 <= n - 2 < n_units:
                    stageB2(n - 2)
                if 0 <= n - 3 < n_units:
                    stageC(n - 3)

        A.reset()
        qlT = A.alloc([8, T], BF16)
        kvlT = A.alloc([4, T], BF16)
        kx = A.alloc([T], BF16)
        ksw = A.alloc([T], BF16)
        kpe2 = A.alloc([T], BF16)
        cos2 = A.alloc([T], F32)
        sinS = A.alloc([T], F32)
        off_grp = A.off
        s_l3 = P.dma_sem("s_l3")
        l_q = P.dma("sp", s_l3, lambda e: e.dma_start(out=qlT, in_=projT[0:8].rearrange("c p t -> p c t")))
        l_kv = P.dma("sp", s_l3, lambda e: e.dma_start(out=kvlT, in_=projT[8:12].rearrange("c p t -> p c t")))
        l_k = [P.dma("sp", s_l3, lambda e: e.dma_start(out=kx[0:64, :], in_=projT[32, 0:64, :])),
               P.dma("sp", s_l3, lambda e: e.dma_start(out=kx[64:128, :], in_=projT[32, 0:64, :])),
               P.dma("sp", s_l3, lambda e: e.dma_start(out=ksw[0:64, :], in_=projT[32, 64:128, :])),
               P.dma("sp", s_l3, lambda e: e.dma_start(out=ksw[64:128, :], in_=projT[32, 64:128, :]))]
        posi = A.alloc([T], I32)
        ang = A.alloc([T], F32)
        kf = A.alloc([T], F32)
        ki = A.alloc([T], I32)
        rbc = A.alloc([T], F32)
        sqb = Ring([A.alloc([T], BF16) for _ in range(2)])
        l_pos = P.dma("sp", s_l3, lambda e: e.dma_start(out=posi, in_=pos_d.partition_broadcast(128)))
        L3 = P.group([l_q, l_kv, l_pos] + l_k)

        def latent_norm(xT, nch, gcol0, ready):
            evs = []
            for c in range(nch):
                si, sb, sfree = sqb.next()
                sq = P.op("act", lambda e, sb=sb, c=c: e.activation(out=sb, in_=xT[:, c, :], func=AF.Square), waits=ready + sfree, sig=True)
                for tb in range(NTB):
                    mm = P.op("pe", lambda e, sb=sb, tb=tb, c=c: e.matmul(ps[:, tb, :], lhsT=onesb, rhs=sb[:, tb * 512:(tb + 1) * 512], start=(c == 0), stop=(c == nch - 1)),
                              waits=[sq, c_ob] if tb == 0 else (), sig=(tb == NTB - 1))
                sqb.release(si, [mm])
            t1 = P.op("dve", lambda e: e.tensor_scalar(out=rbc, in0=ps[:, 0:4, :].rearrange("p b n -> p (b n)"), scalar1=1.0 / (nch * 128), scalar2=EPS, op0=ALU.mult, op1=ALU.add), waits=[mm], sig=True)
            t2 = P.op("act", lambda e: e.activation(out=rbc, in_=rbc, func=AF.Sqrt), waits=[t1], sig=True)
            t3 = P.op("dve", lambda e: e.reciprocal(out=rbc, in_=rbc), waits=[t2], sig=True)
            ev = t3
            for c in range(nch):
                ev = P.op("dve", lambda e, c=c: e.scalar_tensor_tensor(out=xT[:, c, :], in0=xT[:, c, :], scalar=colv[:, gcol0 + c:gcol0 + c + 1], in1=rbc, op0=ALU.mult, op1=ALU.mult),
                          waits=[t3] + cload, sig=True)
            return ev

        n_q = latent_norm(qlT, 8, CV_QG, [l_q])
        n_kv = latent_norm(kvlT, 4, CV_KVG, [l_kv, n_q])

        C1 = 6.28125
        C2 = float(2 * np.pi - 6.28125)
        PI = float(np.pi)
        r0 = P.op("dve", lambda e: e.tensor_copy(out=ang, in_=posi), waits=[l_pos], sig=True)
        r1 = P.op("dve", lambda e: e.tensor_scalar(out=ang, in0=ang, scalar1=colv[:, CV_INV:CV_INV + 1], scalar2=0.0, op0=ALU.mult, op1=ALU.add), waits=[r0] + cload, sig=True)

        def sin_of(dst, shift, prev):
            a = P.op("dve", lambda e: e.tensor_scalar(out=dst, in0=ang, scalar1=shift, scalar2=0.0, op0=ALU.add, op1=ALU.add), waits=prev, sig=True)
            b = P.op("dve", lambda e: e.tensor_scalar(out=kf, in0=dst, scalar1=float(1 / (2 * np.pi)), scalar2=0.0, op0=ALU.mult, op1=ALU.add), waits=[a], sig=True)
            c = P.op("dve", lambda e: e.tensor_copy(out=ki, in_=kf), waits=[b], sig=True)
            d = P.op("dve", lambda e: e.tensor_copy(out=kf, in_=ki), waits=[c], sig=True)
            f = P.op("dve", lambda e: e.scalar_tensor_tensor(out=dst, in0=kf, scalar=-C1, in1=dst, op0=ALU.mult, op1=ALU.add), waits=[d], sig=True)
            g = P.op("dve", lambda e: e.scalar_tensor_tensor(out=dst, in0=kf, scalar=-C2, in1=dst, op0=ALU.mult, op1=ALU.add), waits=[f], sig=True)
            h = P.op("dve", lambda e: e.tensor_scalar(out=kf, in0=dst, scalar1=PI, scalar2=0.0, op0=ALU.is_gt, op1=ALU.add), waits=[g], sig=True)
            i = P.op("dve", lambda e: e.scalar_tensor_tensor(out=dst, in0=kf, scalar=-2 * PI, in1=dst, op0=ALU.mult, op1=ALU.add), waits=[h], sig=True)
            h2 = P.op("dve", lambda e: e.tensor_scalar(out=kf, in0=dst, scalar1=-PI, scalar2=0.0, op0=ALU.is_lt, op1=ALU.add), waits=[i], sig=True)
            i2 = P.op("dve", lambda e: e.scalar_tensor_tensor(out=dst, in0=kf, scalar=2 * PI, in1=dst, op0=ALU.mult, op1=ALU.add), waits=[h2], sig=True)
            k = P.op("dve", lambda e: e.tensor_scalar(out=dst, in0=dst, scalar1=-PI, scalar2=PI, op0=ALU.max, op1=ALU.min), waits=[i2], sig=True)
            return P.op("act", lambda e: e.activation(out=dst, in_=dst, func=AF.Sin), waits=[k], sig=True)

        e_sin = sin_of(sinS, 0.0, [r1, n_kv])
        e_cos = sin_of(cos2, PI / 2, [e_sin])
        e_s1 = P.op("dve", lambda e: e.tensor_scalar(out=sinS[0:32, :], in0=sinS[0:32, :], scalar1=-1.0, scalar2=0.0, op0=ALU.mult, op1=ALU.add), waits=[e_sin, e_cos], sig=True)
        e_s2 = P.op("dve", lambda e: e.tensor_scalar(out=sinS[64:96, :], in0=sinS[64:96, :], scalar1=-1.0, scalar2=0.0, op0=ALU.mult, op1=ALU.add), waits=[e_s1], sig=True)
        e_k1 = P.op("dve", lambda e: e.tensor_tensor(out=ang, in0=kx, in1=cos2, op=ALU.mult), waits=l_k + [e_s2], sig=True)
        e_k2 = P.op("dve", lambda e: e.tensor_tensor(out=kf, in0=ksw, in1=sinS, op=ALU.mult), waits=l_k + [e_k1], sig=True)
        e_kpe = P.op("dve", lambda e: e.tensor_tensor(out=kpe2, in0=ang, in1=kf, op=ALU.add), waits=[e_k2], sig=True)
        setup3 = [e_kpe, n_kv, n_q, e_s2, e_cos]

        A.reset(off_grp)
        qn = A.alloc([4, T], BF16)
        qp = A.alloc([2, T], BF16)
        kn = A.alloc([4, T], BF16)
        vt = A.alloc([NT, 512], BF16)
        wq_n = A.alloc([8, 512], BF16)
        wq_r = A.alloc([8, 256], BF16)
        wq_s = A.alloc([8, 256], BF16)
        wk_k = A.alloc([4, 512], BF16)
        wk_v = A.alloc([4, 512], BF16)
        rt1 = Ring([A.alloc([512], F32) for _ in range(2)])
        rt2 = Ring([A.alloc([512], F32) for _ in range(2)])
        Pb = Ring([A.alloc([T], BF16) for _ in range(2)])
        pTb = Ring([A.alloc([NT, 128], BF16) for _ in range(2)])
        dg = Ring([A.alloc([128], BF16) for _ in range(3)])
        sm = Ring([A.alloc([4], F32) for _ in range(4)])
        ast = Ring([A.alloc([T], BF16) for _ in range(2)])
        s_ast = [P.dma_sem(f"s_ast{i}") for i in range(2)]
        s_wg = P.dma_sem("s_wg")
        ps_sc = ps[:, 0:4, :].rearrange("p b n -> p (b n)")
        ps_t = Ring([ps[:, 4, :], ps[:, 5, :]])
        ps_o = Ring([ps[:, 6, 0:128], ps[:, 7, 0:128]])
        ps_g = Ring([ps[:, 4, :], ps[:, 5, :], ps[:, 6, :], ps[:, 7, :]])
        sc_free = [[]]
        grp_free = setup3
        ast_state = {}

        for g in range(4):
            wl = [P.dma("pool", s_wg, lambda e, g=g: e.dma_start(out=wq_n, in_=wslab(w_uq, 0, 8, g * 512, (g + 1) * 512)), waits=grp_free),
                  P.dma("pool", s_wg, lambda e, g=g: e.dma_start(out=wq_r, in_=wslab(w_uq, 0, 8, 2048 + g * 256, 2048 + (g + 1) * 256))),
                  P.dma("pool", s_wg, lambda e, g=g: e.dma_start(out=wq_s, in_=wslab(w_uq, 0, 8, 3072 + g * 256, 3072 + (g + 1) * 256))),
                  P.dma("pool", s_wg, lambda e, g=g: e.dma_start(out=wk_k, in_=wslab(w_ukv, 0, 4, g * 512, (g + 1) * 512))),
                  P.dma("pool", s_wg, lambda e, g=g: e.dma_start(out=wk_v, in_=wslab(w_ukv, 0, 4, 2048 + g * 512, 2048 + (g + 1) * 512)))]
            P.group(wl)
            gw = wl + grp_free
            gevs = []
            for (dst, wsrc, src, KC) in ((qn, wq_n, qlT, 8), (kn, wk_k, kvlT, 4)):
                for hh in range(4):
                    for tb in range(NTB):
                        bi, pb, pfree = ps_g.next()
                        for kc in range(KC):
                            mm = P.op("pe", lambda e, pb=pb, wsrc=wsrc, src=src, hh=hh, kc=kc, tb=tb, KC=KC: e.matmul(pb, lhsT=wsrc[:, kc, hh * 128:(hh + 1) * 128], rhs=src[:, kc, tb * 512:(tb + 1) * 512], start=(kc == 0), stop=(kc == KC - 1)),
                                      waits=(gw + pfree) if kc == 0 else (), sig=(kc == KC - 1))
                        ev = P.op("act", lambda e, pb=pb, dst=dst, hh=hh, tb=tb: e.activation(out=dst[:, hh, tb * 512:(tb + 1) * 512], in_=pb, func=AF.Copy), waits=[mm], sig=True)
                        ps_g.release(bi, [ev])
                        gevs.append(ev)
            for pc in range(2):
                for tb in range(NTB):
                    b1, pb1, pf1 = ps_g.next()
                    for kc in range(8):
                        mm1 = P.op("pe", lambda e, pb1=pb1, pc=pc, kc=kc, tb=tb: e.matmul(pb1, lhsT=wq_r[:, kc, pc * 128:(pc + 1) * 128], rhs=qlT[:, kc, tb * 512:(tb + 1) * 512], start=(kc == 0), stop=(kc == 7)),
                                    waits=(gw + pf1) if kc == 0 else (), sig=(kc == 7))
                    b2, pb2, pf2 = ps_g.next()
                    for kc in range(8):
                        mm2 = P.op("pe", lambda e, pb2=pb2, pc=pc, kc=kc, tb=tb: e.matmul(pb2, lhsT=wq_s[:, kc, pc * 128:(pc + 1) * 128], rhs=qlT[:, kc, tb * 512:(tb + 1) * 512], start=(kc == 0), stop=(kc == 7)),
                                    waits=(gw + pf2) if kc == 0 else (), sig=(kc == 7))
                    i1, t1b, f1 = rt1.next()
                    i2, t2b, f2 = rt2.next()
                    a1 = P.op("dve", lambda e, t1b=t1b, pb1=pb1, tb=tb: e.tensor_tensor(out=t1b, in0=pb1, in1=cos2[:, tb * 512:(tb + 1) * 512], op=ALU.mult), waits=[mm1] + f1, sig=True)
                    a2 = P.op("dve", lambda e, t2b=t2b, pb2=pb2, tb=tb: e.tensor_tensor(out=t2b, in0=pb2, in1=sinS[:, tb * 512:(tb + 1) * 512], op=ALU.mult), waits=[mm2] + f2, sig=True)
                    a3 = P.op("dve", lambda e, t1b=t1b, t2b=t2b, pc=pc, tb=tb: e.tensor_tensor(out=qp[:, pc, tb * 512:(tb + 1) * 512], in0=t1b, in1=t2b, op=ALU.add), waits=[a1, a2], sig=True)
                    ps_g.release(b1, [a1])
                    ps_g.release(b2, [a2])
                    rt1.release(i1, [a3])
                    rt2.release(i2, [a3])
                    gevs.append(a3)
            for tt in range(NT):
                bi, pb, pfree = ps_g.next()
                for kc in range(4):
                    mm = P.op("pe", lambda e, pb=pb, kc=kc, tt=tt: e.matmul(pb, lhsT=kvlT[:, kc, tt * 128:(tt + 1) * 128], rhs=wk_v[:, kc, :], start=(kc == 0), stop=(kc == 3)),
                              waits=(gw + pfree) if kc == 0 else (), sig=(kc == 3))
                ev = P.op("act", lambda e, pb=pb, tt=tt: e.activation(out=vt[:, tt, :], in_=pb, func=AF.Copy), waits=[mm], sig=True)
                ps_g.release(bi, [ev])
                gevs.append(ev)
            grp_ready = gevs

            units = [(hh, i) for hh in range(4) for i in range(NT)]
            U = {}

            def stA(n, g=g, units=units, U=U, grp_ready=grp_ready):
                hh, i = units[n]
                L = (i + 1) * 128
                r0_ = (hh % 2) * 64
                first = True
                segs = [(s0, min(512, i * 128 - s0)) for s0 in range(0, i * 128, 512)]
                mm = None
                for (s0, n_) in segs:
                    P.op("pe", lambda e, hh=hh, i=i, s0=s0, n_=n_: e.matmul(ps_sc[:, s0:s0 + n_], lhsT=qn[:, hh, i * 128:(i + 1) * 128], rhs=kn[:, hh, s0:s0 + n_], start=True, stop=False),
                         waits=(grp_ready + sc_free[0]) if first else ())
                    first = False
                    mm = P.op("pe", lambda e, hh=hh, i=i, s0=s0, n_=n_, r0_=r0_: e.matmul(ps_sc[:, s0:s0 + n_], lhsT=qp[r0_:r0_ + 64, hh // 2, i * 128:(i + 1) * 128], rhs=kpe2[r0_:r0_ + 64, s0:s0 + n_], start=False, stop=True))
                d0 = i * 128
                P.op("pe", lambda e, hh=hh, i=i, d0=d0: e.matmul(ps_sc[:, d0:d0 + 128], lhsT=qn[:, hh, i * 128:(i + 1) * 128], rhs=kn[:, hh, d0:d0 + 128], start=True, stop=False),
                     waits=(grp_ready + sc_free[0]) if first else ())
                P.op("pe", lambda e, hh=hh, i=i, d0=d0, r0_=r0_: e.matmul(ps_sc[:, d0:d0 + 128], lhsT=qp[r0_:r0_ + 64, hh // 2, i * 128:(i + 1) * 128], rhs=kpe2[r0_:r0_ + 64, d0:d0 + 128], start=False, stop=False))
                mm = P.op("pe", lambda e, d0=d0: e.matmul(ps_sc[:, d0:d0 + 128], lhsT=mrowb[0:1, 0:128], rhs=mrowb[0:1, 128:256], start=False, stop=True), waits=mod_ready, sig=True)
                U[n] = {"S": mm, "L": L}

            def stB(n, units=units, U=U):
                hh, i = units[n]
                L = U[n]["L"]
                si, sb, sfree = sm.next()
                mx = P.op("dve", lambda e, sb=sb, L=L: e.reduce_max(out=sb[:, 0:1], in_=ps_sc[:, 0:L], axis=AX.X), waits=[U[n]["S"]] + sfree, sig=True)
                nb_ = P.op("dve", lambda e, sb=sb: e.tensor_scalar(out=sb[:, 1:2], in0=sb[:, 0:1], scalar1=-MLA_SCALE, scalar2=0.0, op0=ALU.mult, op1=ALU.add), waits=[mx], sig=True)
                pi_, pb_, pfree = Pb.next()
                ex = P.op("act", lambda e, sb=sb, pb_=pb_, L=L: e.activation(out=pb_[:, 0:L], in_=ps_sc[:, 0:L], func=AF.Exp, bias=sb[:, 1:2], scale=MLA_SCALE, accum_out=sb[:, 2:3]), waits=[nb_] + pfree, sig=True)
                sc_free[0] = [ex]
                ri = P.op("dve", lambda e, sb=sb: e.reciprocal(out=sb[:, 3:4], in_=sb[:, 2:3]), waits=[ex], sig=True)
                di, db, dfree = dg.next()
                dgv = P.op("dve", lambda e, sb=sb, db=db: e.tensor_scalar(out=db, in0=identb, scalar1=sb[:, 3:4], scalar2=0.0, op0=ALU.mult, op1=ALU.add), waits=[ri, c_ib] + dfree, sig=True)
                U[n].update({"ex": ex, "dg": dgv, "pi": pi_, "pb": pb_, "di": di, "db": db, "si": si})

            def stC(n, g=g, units=units, U=U):
                hh, i = units[n]
                u = U[n]
                pb_, db = u["pb"], u["db"]
                ti, tbuf, tfree = pTb.next()
                nblk = i + 1
                evs = []
                lastT = None
                for b0 in range(0, nblk, 4):
                    nb4 = min(4, nblk - b0)
                    bi, pt, ptfree = ps_t.next()
                    for q in range(nb4):
                        kb = b0 + q
                        lastT = P.op("pe", lambda e, pt=pt, q=q, kb=kb, pb_=pb_, db=db: e.matmul(pt[:, q * 128:(q + 1) * 128], lhsT=pb_[:, kb * 128:(kb + 1) * 128], rhs=db, start=True, stop=True),
                                     waits=([u["ex"], u["dg"]] + ptfree) if q == 0 else (), sig=(q == nb4 - 1))
                    ev = P.op("act", lambda e, pt=pt, b0=b0, nb4=nb4, tbuf=tbuf: e.activation(out=tbuf[:, b0:b0 + nb4, :].rearrange("p a b -> p (a b)"), in_=pt[:, 0:nb4 * 128], func=AF.Copy),
                              waits=[lastT] + (tfree if b0 == 0 else []), sig=True)
                    ps_t.release(bi, [ev])
                    evs.append(ev)
                Pb.release(u["pi"], [lastT])
                dg.release(u["di"], [lastT])
                sm.release(u["si"], [u["dg"]])
                oi, po, pofree = ps_o.next()
                for kb in range(nblk):
                    pv = P.op("pe", lambda e, po=po, kb=kb, hh=hh, tbuf=tbuf, nblk=nblk: e.matmul(po, lhsT=vt[:, kb, hh * 128:(hh + 1) * 128], rhs=tbuf[:, kb, :], start=(kb == 0), stop=(kb == nblk - 1)),
                              waits=(evs + pofree) if kb == 0 else (), sig=(kb == nblk - 1))
                pTb.release(ti, [pv])
                u["oi"], u["po"], u["pv"] = oi, po, pv

            def stC2(n, g=g, units=units, U=U):
                hh, i = units[n]
                u = U[n]
                oi, po, pv = u["oi"], u["po"], u["pv"]
                if i == 0:
                    ast_state["i"], ast_state["buf"], ast_state["free"] = ast.next()
                    ast_state["evs"] = []
                abuf = ast_state["buf"]
                oe = P.op("dve", lambda e, po=po, abuf=abuf, i=i: e.tensor_copy(out=abuf[:, i * 128:(i + 1) * 128], in_=po), waits=[pv] + (ast_state["free"] if i == 0 else []), sig=True)
                ps_o.release(oi, [oe])
                ast_state["evs"].append(oe)
                U[n]["pv"] = pv
                U[n]["oe"] = oe
                if i == NT - 1:
                    ai = ast_state["i"]
                    h = g * 4 + hh
                    st = P.dma("sp", s_ast[ai], lambda e, abuf=abuf, h=h: e.dma_start(out=attnT[h], in_=abuf), waits=ast_state["evs"])
                    ast.release(ai, [st])

            run_skewed(len(units), stA, stB, stC, single_buf=True, stageC2=stC2)
            grp_free = [U[k][w] for k in range(len(units) - 4, len(units)) for w in ("pv", "oe")]
        P.barrier()
        if stop_after <= 3:
            P.run_block()
            return nc

        A.reset()
        qsT = A.alloc([16, T], BF16)
        kdup = A.alloc([4, T], BF16)
        vpd = A.alloc([NT, 1024], BF16)
        bm = A.alloc([32, 256], F32)
        mk = A.alloc([256], F32)
        snk = A.alloc([32], F32)
        ssb = Ring([A.alloc([256], F32) for _ in range(4)])
        Pb4 = Ring([A.alloc([256], BF16) for _ in range(4)])
        pT4 = Ring([A.alloc([2, 128], BF16) for _ in range(4)])
        dg4 = Ring([A.alloc([128], BF16) for _ in range(4)])
        sm4 = Ring([A.alloc([8], F32) for _ in range(6)])
        bst = Ring([A.alloc([T], BF16) for _ in range(2)])
        s_bst = [P.dma_sem(f"s_bst{i}") for i in range(2)]
        s_l4 = P.dma_sem("s_l4")
        L4 = [P.dma("sp", s_l4, lambda e: e.dma_start(out=qsT, in_=projT[12:28].rearrange("c p t -> p c t"))),
              P.dma("sp", s_l4, lambda e: e.dma_start(out=vpd, in_=vpad_d.rearrange("(tt p) f -> p tt f", p=128))),
              P.dma("sp", s_l4, lambda e: e.dma_start(out=bm, in_=biasg_d.rearrange("h q k -> q h k"))),
              P.dma("sp", s_l4, lambda e: e.dma_start(out=mk, in_=maskc_d)),
              P.dma("sp", s_l4, lambda e: e.dma_start(out=snk, in_=sinks_d.partition_broadcast(128)))]
        for gk in range(4):
            for half in range(2):
                L4.append(P.dma("sp", s_l4, lambda e, gk=gk, half=half: e.dma_start(out=kdup[half * 64:(half + 1) * 64, gk, :], in_=projT[28 + gk // 2, (gk % 2) * 64:(gk % 2) * 64 + 64, :])))
        P.group(L4)
        bme = None
        for h in range(32):
            bme = P.op("dve", lambda e, h=h: e.tensor_tensor(out=bm[:, h, :], in0=bm[:, h, :], in1=mk, op=ALU.add), waits=L4 if h == 0 else (), sig=(h == 31))
        ready4 = L4 + [bme]
        ps_s4 = Ring([ps[:, 0, 0:256], ps[:, 1, 0:256], ps[:, 2, 0:256]])
        ps_t4 = Ring([ps[:, 3, 0:256], ps[:, 4, 0:256]])
        ps_o4 = Ring([ps[:, 5, 0:128], ps[:, 6, 0:128]])
        units4 = [(j, nb, hx) for j in range(16) for nb in range(NT) for hx in range(2)]
        U4 = {}
        ost = {}

        def s4A(n):
            j, nb, hx = units4[n]
            h = 2 * j + hx
            gk = j // 4
            r0_ = hx * 64
            k0 = (nb - 1) * 128 if nb > 0 else 0
            nk = 256 if nb > 0 else 128
            bi, pss, pfree = ps_s4.next()
            mm = P.op("pe", lambda e: e.matmul(pss[:, 0:nk], lhsT=qsT[r0_:r0_ + 64, j, nb * 128:(nb + 1) * 128], rhs=kdup[r0_:r0_ + 64, gk, k0:k0 + nk], start=True, stop=True),
                      waits=ready4 + pfree, sig=True)
            U4[n] = {"S": mm, "bi": bi, "pss": pss, "nk": nk, "h": h, "gk": gk}

        def s4B(n):
            j, nb, hx = units4[n]
            u = U4[n]
            nk, h, pss = u["nk"], u["h"], u["pss"]
            c0 = 0 if nb > 0 else 128
            si, sb, sfree = ssb.next()
            mi, st_, mfree = sm4.next()
            a = P.op("dve", lambda e: e.scalar_tensor_tensor(out=sb[:, 0:nk], in0=pss[:, 0:nk], scalar=SWA_SCALE, in1=bm[:, h, c0:c0 + nk], op0=ALU.mult, op1=ALU.add), waits=[u["S"]] + sfree, sig=True)
            ps_s4.release(u["bi"], [a])
            mx = P.op("dve", lambda e: e.reduce_max(out=st_[:, 0:1], in_=sb[:, 0:nk], axis=AX.X), waits=[a] + mfree, sig=True)
            ng_ = P.op("dve", lambda e: e.tensor_scalar(out=st_[:, 1:2], in0=st_[:, 0:1], scalar1=snk[:, h:h + 1], scalar2=-1.0, op0=ALU.max, op1=ALU.mult), waits=[mx], sig=True)
            pi_, pb_, pfree = Pb4.next()
            ex = P.op("act", lambda e: e.activation(out=pb_[:, 0:nk], in_=sb[:, 0:nk], func=AF.Exp, bias=st_[:, 1:2], scale=1.0, accum_out=st_[:, 2:3]), waits=[ng_] + pfree, sig=True)
            es_ = P.op("act", lambda e: e.activation(out=st_[:, 3:4], in_=snk[:, h:h + 1], func=AF.Exp, bias=st_[:, 1:2], scale=1.0), waits=[ng_], sig=True)
            ssb.release(si, [ex])
            u.update({"ex": ex, "es": es_, "pi": pi_, "pb": pb_, "mi": mi, "st": st_})

        def s4B2(n):
            u = U4[n]
            ex, es_, st_, mi, pi_, pb_ = u["ex"], u["es"], u["st"], u["mi"], u["pi"], u["pb"]
            dn = P.op("dve", lambda e: e.tensor_tensor(out=st_[:, 4:5], in0=st_[:, 2:3], in1=st_[:, 3:4], op=ALU.add), waits=[ex, es_], sig=True)
            ri = P.op("dve", lambda e: e.reciprocal(out=st_[:, 5:6], in_=st_[:, 4:5]), waits=[dn], sig=True)
            di, db, dfree = dg4.next()
            dgv = P.op("dve", lambda e: e.tensor_scalar(out=db, in0=identb, scalar1=st_[:, 5:6], scalar2=0.0, op0=ALU.mult, op1=ALU.add), waits=[ri] + dfree, sig=True)
            sm4.release(mi, [dgv])
            u.update({"ex": ex, "dg": dgv, "pi": pi_, "pb": pb_, "di": di, "db": db})

        def s4C(n):
            j, nb, hx = units4[n]
            u = U4[n]
            nk, gk, pb_, db = u["nk"], u["gk"], u["pb"], u["db"]
            nblk = nk // 128
            bi, pt, ptfree = ps_t4.next()
            for kb in range(nblk):
                lastT = P.op("pe", lambda e, kb=kb: e.matmul(pt[:, kb * 128:(kb + 1) * 128], lhsT=pb_[:, kb * 128:(kb + 1) * 128], rhs=db, start=True, stop=True),
                             waits=([u["ex"], u["dg"]] + ptfree) if kb == 0 else (), sig=(kb == nblk - 1))
            Pb4.release(u["pi"], [lastT])
            dg4.release(u["di"], [lastT])
            ti, tbuf, tfree = pT4.next()
            ev = P.op("act", lambda e: e.activation(out=tbuf[:, 0:nblk, :].rearrange("p a b -> p (a b)"), in_=pt[:, 0:nk], func=AF.Copy), waits=[lastT] + tfree, sig=True)
            ps_t4.release(bi, [ev])
            if hx == 0:
                ost["oi"], ost["po"], ost["pofree"] = ps_o4.next()
            po = ost["po"]
            tt0 = nb - 1 if nb > 0 else 0
            for kb in range(nblk):
                pv = P.op("pe", lambda e, kb=kb: e.matmul(po, lhsT=vpd[:, tt0 + kb, (gk * 2 + hx) * 128:(gk * 2 + hx + 1) * 128], rhs=tbuf[:, kb, :], start=(hx == 0 and kb == 0), stop=(hx == 1 and kb == nblk - 1)),
                          waits=([ev] + (ost["pofree"] if hx == 0 else [])) if kb == 0 else (), sig=(kb == nblk - 1))
            pT4.release(ti, [pv])
            if hx == 1:
                if nb == 0:
                    ost["si"], ost["sbuf"], ost["sfree"] = bst.next()
                    ost["evs"] = []
                sbuf = ost["sbuf"]
                oe = P.op("dve", lambda e: e.tensor_copy(out=sbuf[:, nb * 128:(nb + 1) * 128], in_=po), waits=[pv] + (ost["sfree"] if nb == 0 else []), sig=True)
                ps_o4.release(ost["oi"], [oe])
                ost["evs"].append(oe)
                if nb == NT - 1:
                    si_ = ost["si"]
                    st = P.dma("sp", s_bst[si_], lambda e: e.dma_start(out=attnT[16 + j], in_=sbuf), waits=ost["evs"])
                    bst.release(si_, [st])

        run_skewed(len(units4), s4A, s4B, s4C, stageB2=s4B2)
        P.barrier()
        if stop_after <= 4:
            P.run_block()
            return nc

        A.reset()
        aT = A.alloc([16, T], BF16)
        bT = A.alloc([16, T], BF16)
        wa5 = Ring([A.alloc([16, 256], BF16) for _ in range(2)])
        wb5 = Ring([A.alloc([16, 256], BF16) for _ in range(2)])
        ga5 = Ring([A.alloc([T], BF16) for _ in range(2)])
        gb5 = Ring([A.alloc([T], BF16) for _ in range(2)])
        t15 = Ring([A.alloc([512], F32) for _ in range(2)])
        t25 = Ring([A.alloc([512], F32) for _ in range(2)])
        zst = Ring([A.alloc([T], BF16) for _ in range(2)])
        s_wa5 = [P.dma_sem(f"s_wa5{i}") for i in range(2)]
        s_wb5 = [P.dma_sem(f"s_wb5{i}") for i in range(2)]
        s_ga5 = [P.dma_sem(f"s_ga5{i}") for i in range(2)]
        s_gb5 = [P.dma_sem(f"s_gb5{i}") for i in range(2)]
        s_zst = [P.dma_sem(f"s_zst{i}") for i in range(2)]
        s_l5 = P.dma_sem("s_l5")
        L5 = [P.dma("sp", s_l5, lambda e: e.dma_start(out=aT, in_=attnT[0:16].rearrange("c p t -> p c t"))),
              P.dma("sp", s_l5, lambda e: e.dma_start(out=bT, in_=attnT[16:32].rearrange("c p t -> p c t")))]
        P.group(L5)
        ps_a5 = Ring([ps[:, 0, :], ps[:, 2, :], ps[:, 4, :], ps[:, 6, :]])
        ps_b5 = Ring([ps[:, 1, :], ps[:, 3, :], ps[:, 5, :], ps[:, 7, :]])
        for s in range(16):
            wai, wab, waf = wa5.next()
            wbi, wbb, wbf = wb5.next()
            lda = P.dma("pool", s_wa5[wai], lambda e, wab=wab, s=s: e.dma_start(out=wab, in_=wslab(w_pa, 0, 16, s * 256, (s + 1) * 256)), waits=waf)
            ldb = P.dma("pool", s_wb5[wbi], lambda e, wbb=wbb, s=s: e.dma_start(out=wbb, in_=wslab(w_pb, 0, 16, s * 256, (s + 1) * 256)), waits=wbf)
            for cc in range(2):
                n = s * 2 + cc
                gai, gab, gaf = ga5.next()
                gbi, gbb, gbf = gb5.next()
                lga = P.dma("sp", s_ga5[gai], lambda e, gab=gab, n=n: e.dma_start(out=gab, in_=gatesT[n]), waits=gaf)
                lgb = P.dma("sp", s_gb5[gbi], lambda e, gbb=gbb, n=n: e.dma_start(out=gbb, in_=gatesT[32 + n]), waits=gbf)
                zi, zb, zf = zst.next()
                zevs = []
                for tb in range(NTB):
                    ai, pa, paf = ps_a5.next()
                    for kc in range(16):
                        mma = P.op("pe", lambda e, pa=pa, wab=wab, cc=cc, kc=kc, tb=tb: e.matmul(pa, lhsT=wab[:, kc, cc * 128:(cc + 1) * 128], rhs=aT[:, kc, tb * 512:(tb + 1) * 512], start=(kc == 0), stop=(kc == 15)),
                                   waits=([lda] + L5 + paf) if kc == 0 else (), sig=(kc == 15))
                    bi_, pb5, pbf = ps_b5.next()
                    for kc in range(16):
                        mmb = P.op("pe", lambda e, pb5=pb5, wbb=wbb, cc=cc, kc=kc, tb=tb: e.matmul(pb5, lhsT=wbb[:, kc, cc * 128:(cc + 1) * 128], rhs=bT[:, kc, tb * 512:(tb + 1) * 512], start=(kc == 0), stop=(kc == 15)),
                                   waits=([ldb] + L5 + pbf) if kc == 0 else (), sig=(kc == 15))
                    i1, t1b, f1 = t15.next()
                    i2, t2b, f2 = t25.next()
                    a1 = P.op("dve", lambda e, t1b=t1b, pa=pa, gab=gab, tb=tb: e.tensor_tensor(out=t1b, in0=pa, in1=gab[:, tb * 512:(tb + 1) * 512], op=ALU.mult), waits=[mma, lga] + f1, sig=True)
                    a2 = P.op("dve", lambda e, t2b=t2b, pb5=pb5, gbb=gbb, tb=tb: e.tensor_tensor(out=t2b, in0=pb5, in1=gbb[:, tb * 512:(tb + 1) * 512], op=ALU.mult), waits=[mmb, lgb] + f2, sig=True)
                    a3 = P.op("dve", lambda e, t1b=t1b, t2b=t2b, zb=zb, tb=tb: e.tensor_tensor(out=zb[:, tb * 512:(tb + 1) * 512], in0=t1b, in1=t2b, op=ALU.add), waits=[a1, a2] + (zf if tb == 0 else []), sig=True)
                    ps_a5.release(ai, [a1])
                    ps_b5.release(bi_, [a2])
                    t15.release(i1, [a3])
                    t25.release(i2, [a3])
                    zevs.append(a3)
                ga5.release(gai, [a1])
                gb5.release(gbi, [a2])
                st = P.dma("sp", s_zst[zi], lambda e, zb=zb, n=n: e.dma_start(out=zT[n], in_=zb), waits=zevs)
                zst.release(zi, [st])
            wa5.release(wai, [mma])
            wb5.release(wbi, [mmb])
        P.barrier()
        if stop_after <= 5:
            P.run_block()
            return nc

        def build_bc(gvcol, gvbc, ready):
            dgf = Ring([A.alloc([128], F32) for _ in range(2)])
            ev = None
            for g4 in range(8):
                bi, pb, pfree = ps_ring8.next()
                for q in range(4):
                    c = g4 * 4 + q
                    di, db, dfree = dgf.next()
                    dv = P.op("dve", lambda e, db=db, c=c: e.tensor_scalar(out=db, in0=identf, scalar1=gvcol[:, c:c + 1], scalar2=0.0, op0=ALU.mult, op1=ALU.add), waits=ready + dfree + cload, sig=True)
                    mm = P.op("pe", lambda e, pb=pb, db=db, q=q: e.matmul(pb[:, q * 128:(q + 1) * 128], lhsT=onesf, rhs=db, start=True, stop=True), waits=[dv, c_of] + (pfree if q == 0 else []), sig=True)
                    dgf.release(di, [mm])
                ev = P.op("act", lambda e, pb=pb, g4=g4: e.activation(out=gvbc[:, g4 * 512:(g4 + 1) * 512], in_=pb, func=AF.Copy), waits=[mm], sig=True)
                ps_ring8.release(bi, [ev])
            return ev

        def gemm_a_acc(inT_d, KCtot, W, gvcol, s_pref):
            A.reset()
            acc = A.alloc([NT, 1024], F32)
            ug = Ring([A.alloc([8, T], BF16) for _ in range(2)])
            wg = Ring([A.alloc([8, 1024], BF16) for _ in range(2)])
            gvbc = A.alloc([D], F32)
            yst = Ring([A.alloc([1024], BF16) for _ in range(2)])
            jk = A.alloc([1024], BF16)
            s_ug = [P.dma_sem(f"{s_pref}_ug{i}") for i in range(2)]
            s_wg_ = [P.dma_sem(f"{s_pref}_wg{i}") for i in range(2)]
            s_yst = [P.dma_sem(f"{s_pref}_yst{i}") for i in range(2)]
            bc_ready = build_bc(gvcol, gvbc, mod_ready)
            NKG = KCtot // 8
            acc_free = [[] for _ in range(NT)]
            acc_last = [[None, None] for _ in range(NT)]
            jk_free = []
            for ng in range(4):
                for kg in range(NKG):
                    ui, ub, uf = ug.next()
                    wi_, wb_, wf = wg.next()
                    lu = P.dma("sp", s_ug[ui], lambda e, ub=ub, kg=kg: e.dma_start(out=ub, in_=inT_d[kg * 8:(kg + 1) * 8].rearrange("c p t -> p c t")), waits=uf)
                    lw = P.dma("pool", s_wg_[wi_], lambda e, wb_=wb_, kg=kg, ng=ng: e.dma_start(out=wb_, in_=wslab(W, kg * 8, (kg + 1) * 8, ng * 1024, (ng + 1) * 1024)), waits=wf)
                    for tt in range(NT):
                        for nn in range(2):
                            bi, pb, pfree = ps_ring8.next()
                            for c in range(8):
                                mm = P.op("pe", lambda e, pb=pb, ub=ub, wb_=wb_, c=c, tt=tt, nn=nn: e.matmul(pb, lhsT=ub[:, c, tt * 128:(tt + 1) * 128], rhs=wb_[:, c, nn * 512:(nn + 1) * 512], start=(c == 0), stop=(c == 7)),
                                          waits=([lu, lw] + pfree) if c == 0 else (), sig=(c == 7))
                            dst = acc[:, tt, nn * 512:(nn + 1) * 512]
                            if kg == 0:
                                ev = P.op("act", lambda e, dst=dst, pb=pb: e.activation(out=dst, in_=pb, func=AF.Copy), waits=[mm] + acc_free[tt], sig=True)
                            else:
                                ev = P.op("dve", lambda e, dst=dst, pb=pb: e.tensor_tensor(out=dst, in0=pb, in1=dst, op=ALU.add), waits=[mm, acc_last[tt][nn]], sig=True)
                            acc_last[tt][nn] = ev
                            ps_ring8.release(bi, [ev])
                    ug.release(ui, [mm])
                    wg.release(wi_, [mm])
                for tt in range(NT):
                    w_acc = [acc_last[tt][0], acc_last[tt][1]]
                    sq = P.op("act", lambda e, tt=tt, ng=ng: e.activation(out=jk, in_=acc[:, tt, :], func=AF.Square, accum_out=ssqp[:, tt, ng:ng + 1]), waits=w_acc + jk_free, sig=True)
                    jk_free = [sq]
                    yi, yb, yf = yst.next()
                    ml = P.op("dve", lambda e, tt=tt, ng=ng, yb=yb: e.tensor_tensor(out=yb, in0=acc[:, tt, :], in1=gvbc[:, ng * 1024:(ng + 1) * 1024], op=ALU.mult), waits=w_acc + [bc_ready] + yf, sig=True)
                    st = P.dma("sp", s_yst[yi], lambda e, tt=tt, ng=ng, yb=yb: e.dma_start(out=ytok[tt * 128:(tt + 1) * 128, ng * 1024:(ng + 1) * 1024], in_=yb), waits=[ml])
                    yst.release(yi, [st])
                    acc_free[tt] = [sq, ml]
            r1_ = P.op("dve", lambda e: e.reduce_sum(out=rstdA, in_=ssqp, axis=AX.X), waits=[sq], sig=True)
            r2_ = P.op("dve", lambda e: e.tensor_scalar(out=rstdA, in0=rstdA, scalar1=1.0 / D, scalar2=EPS, op0=ALU.mult, op1=ALU.add), waits=[r1_], sig=True)
            r3_ = P.op("act", lambda e: e.activation(out=rstdA, in_=rstdA, func=AF.Sqrt), waits=[r2_], sig=True)
            r4_ = P.op("dve", lambda e: e.reciprocal(out=rstdA, in_=rstdA), waits=[r3_], sig=True)
            P.barrier()

        gemm_a_acc(zT, 32, w_out, gv1, "p6")
        if stop_after <= 6:
            P.run_block()
            return nc

        A.reset()
        hT = A.alloc([32, T], BF16)
        off_after_hT = A.off
        xts = Ring([A.alloc([D], F32) for _ in range(2)])
        yts = Ring([A.alloc([D], BF16) for _ in range(2)])
        xh = A.alloc([D], F32)
        s_x7 = [P.dma_sem(f"s_x7{i}") for i in range(2)]
        s_y7 = [P.dma_sem(f"s_y7{i}") for i in range(2)]
        s_o7 = [P.dma_sem(f"s_o7{i}") for i in range(2)]
        xh_free = []
        for tt in range(NT):
            xi, xt, xfree = xts.next()
            yi, yt, yfree = yts.next()
            lx = P.dma("sp", s_x7[xi], lambda e, xt=xt, tt=tt: e.dma_start(out=xt, in_=x_d[tt * 128:(tt + 1) * 128, :]), waits=xfree)
            ly = P.dma("sp", s_y7[yi], lambda e, yt=yt, tt=tt: e.dma_start(out=yt, in_=ytok[tt * 128:(tt + 1) * 128, :]), waits=yfree)
            x1e = P.op("dve", lambda e, xt=xt, yt=yt, tt=tt: e.scalar_tensor_tensor(out=xt, in0=yt, scalar=rstdA[:, tt:tt + 1], in1=xt, op0=ALU.mult, op1=ALU.add), waits=[lx, ly], sig=True)
            so = P.dma("sp", s_o7[xi], lambda e, xt=xt, tt=tt: e.dma_start(out=out_d[tt * 128:(tt + 1) * 128, :], in_=xt), waits=[x1e])
            rd, xhf, ev = norm_transpose(tt, xt, [x1e], yt, [x1e], xh, xh_free, gs2, sh2, hT)
            xts.release(xi, rd + [so])
            yts.release(yi, [rd[0]])
            xh_free = xhf
        P.barrier()
        if stop_after <= 7:
            P.run_block()
            return nc

        A.reset(off_after_hT)
        w8 = Ring([A.alloc([32, 256], BF16) for _ in range(2)])
        s_w8 = [P.dma_sem(f"s_w8_{i}") for i in range(2)]
        ust = Ring([A.alloc([T], BF16) for _ in range(3)])
        s_ust = [P.dma_sem(f"s_ust{i}") for i in range(3)]
        rl = Ring([A.alloc([512], F32) for _ in range(3)])
        cur8 = {}

        def epi8(ci, tb, pb, mm):
            if tb == 0:
                cur8["i"], cur8["buf"], cur8["free"] = ust.next()
                cur8["evs"] = []
            buf = cur8["buf"]
            ri, rb, rf = rl.next()
            ev = P.op("act", lambda e: e.activation(out=rb, in_=pb, func=AF.Relu), waits=[mm] + rf, sig=True)
            e2 = P.op("dve", lambda e: e.tensor_tensor(out=buf[:, tb * 512:(tb + 1) * 512], in0=rb, in1=rb, op=ALU.mult), waits=[ev] + (cur8["free"] if tb == 0 else []), sig=True)
            rl.release(ri, [e2])
            cur8["evs"].append(e2)
            if tb == NTB - 1:
                i = cur8["i"]
                st = P.dma("sp", s_ust[i], lambda e: e.dma_start(out=uT[ci], in_=buf), waits=cur8["evs"])
                ust.release(i, [st])
            return ev

        gemm_b("w_up", hT, 32, w_up, 0, 128, w8, s_w8, epi8)
        P.barrier()
        if stop_after <= 8:
            P.run_block()
            return nc

        gemm_a_acc(uT, 128, w_dn, gv2, "p9")

        A.reset()
        xts = Ring([A.alloc([D], F32) for _ in range(3)])
        yts = Ring([A.alloc([D], BF16) for _ in range(3)])
        s_xa = [P.dma_sem(f"s_xa{i}") for i in range(3)]
        s_ya = [P.dma_sem(f"s_ya{i}") for i in range(3)]
        s_oa = [P.dma_sem(f"s_oa{i}") for i in range(3)]
        for tt in range(NT):
            xi, xt, xfree = xts.next()
            yi, yt, yfree = yts.next()
            lx = P.dma("sp", s_xa[xi], lambda e, xt=xt, tt=tt: e.dma_start(out=xt, in_=out_d[tt * 128:(tt + 1) * 128, :]), waits=xfree)
            ly = P.dma("sp", s_ya[yi], lambda e, yt=yt, tt=tt: e.dma_start(out=yt, in_=ytok[tt * 128:(tt + 1) * 128, :]), waits=yfree)
            fe = P.op("dve", lambda e, xt=xt, yt=yt, tt=tt: e.scalar_tensor_tensor(out=xt, in0=yt, scalar=rstdA[:, tt:tt + 1], in1=xt, op0=ALU.mult, op1=ALU.add), waits=[lx, ly], sig=True)
            so = P.dma("sp", s_oa[xi], lambda e, xt=xt, tt=tt: e.dma_start(out=out_d[tt * 128:(tt + 1) * 128, :], in_=xt), waits=[fe])
            xts.release(xi, [so])
            yts.release(yi, [fe])
        P.barrier()
        P.run_block()
    return nc


def _t5_bucket_np(rel):
    nb = 16
    max_exact = 8
    ret = np.where(rel > 0, nb, 0)
    n = np.abs(rel)
    nf = np.maximum(n, 1).astype(np.float32)
    large = max_exact + (np.log(nf / max_exact) / np.log(128 / max_exact) * (nb - max_exact)).astype(np.int32)
    large = np.minimum(large, nb - 1)
    return ret + np.where(n < max_exact, n, large)


def _host_prep(inp):
    f32 = np.float32
    shared = {}
    w_in = inp["w_in"][0]
    kr = w_in[:, 1536:1600]
    shared["w_in_p"] = np.ascontiguousarray(np.concatenate(
        [w_in[:, 0:1024], w_in[:, 1024:1536], w_in[:, 1600:3648], w_in[:, 3648:3904], w_in[:, 3904:4160],
         kr, kr[:, 32:64], kr[:, 0:32]], axis=1), dtype=f32)
    w_uq = inp["w_uq"][0].reshape(1024, 16, 192)
    nope = w_uq[:, :, 0:128].reshape(1024, 2048)
    rope = w_uq[:, :, 128:192]
    rope_sw = np.concatenate([rope[:, :, 32:64], rope[:, :, 0:32]], axis=2)
    shared["w_uq_p"] = np.ascontiguousarray(np.concatenate([nope, rope.reshape(1024, 1024), rope_sw.reshape(1024, 1024)], axis=1), dtype=f32)
    w_ukv = inp["w_ukv"][0].reshape(512, 16, 256)
    shared["w_ukv_p"] = np.ascontiguousarray(np.concatenate([w_ukv[:, :, 0:128].reshape(512, 2048), w_ukv[:, :, 128:256].reshape(512, 2048)], axis=1), dtype=f32)
    for k in ("w_ada", "w_gate", "w_proj_a", "w_proj_b", "w_out", "w_ff_up", "w_ff_down"):
        shared[k] = np.ascontiguousarray(inp[k][0], dtype=f32)

    def col(v):
        return np.asarray(v, dtype=f32).reshape(-1, 128).T

    inv = (np.float32(10000.0) ** (-np.arange(32, dtype=np.float32) * np.float32(2.0 / 64))).astype(f32)
    colv = np.concatenate([col(inp["pre_norm_g"][0, 0]), col(inp["pre_norm_g"][0, 1]), col(inp["post_norm_g"][0, 0]), col(inp["post_norm_g"][0, 1]),
                           col(inp["q_norm_g"][0]), col(inp["kv_norm_g"][0]), col(inp["b_ada"][0]),
                           np.tile(inv, 4).reshape(128, 1)], axis=1)
    assert colv.shape == (128, CV_N)
    shared["colv"] = np.ascontiguousarray(colv, dtype=f32)
    shared["sinks"] = np.ascontiguousarray(inp["swa_sinks"][0].reshape(1, 32), dtype=f32)
    iq = np.arange(128)
    ik = np.arange(256) - 128
    rel = ik[None, :] - iq[:, None]
    bidx = _t5_bucket_np(rel)
    shared["biasg"] = np.ascontiguousarray(np.transpose(np.asarray(inp["rel_bias"], dtype=f32)[bidx], (2, 0, 1)))
    qh = iq[:, None] // 64
    kh = np.arange(256)[None, :] // 64
    valid = (kh >= qh) & (kh <= qh + 2)
    shared["maskc"] = np.where(valid, 0.0, NEG).astype(f32)
    shared["identf"] = np.eye(128, dtype=f32)
    mrow = np.zeros((1, 256), dtype=f32)
    mrow[0, 0:64] = 1.0
    mrow[0, 128 + 64:256] = -30000.0
    shared["mrow"] = mrow
    in_maps = []
    for b in range(8):
        m = dict(shared)
        m["x"] = np.ascontiguousarray(inp["x"][b], dtype=f32)
        m["cT"] = np.ascontiguousarray(col(inp["c"][b]))
        m["pos"] = np.ascontiguousarray(inp["positions"][b].reshape(1, T).astype(np.int32))
        in_maps.append(m)
    return in_maps


_NC_CACHE = {}


def kernel(**inputs):
    inp = {k: np.asarray(v) for k, v in inputs.items()}
    in_maps = _host_prep(inp)
    if "nc" not in _NC_CACHE:
        _NC_CACHE["nc"] = build_nc()
    res = run_bass_kernel_spmd(_NC_CACHE["nc"], in_maps, core_ids=list(range(8)))
    out = np.stack([np.asarray(r["out"], dtype=np.float32) for r in res.results], axis=0)
    return out
```
